# Optimizing a Trainium2 kernel written in Bass

```python
import jax, jax.numpy as jnp
from jax import lax
import numpy as np

D_MODEL = 1024
BATCH = 2
SEQ = 8192
DEPTH = 4
DEC_BATCH = 128
DEC_SEQ = 8
PAST_LEN = 8192
PAGE_SIZE = 128

HEAD_DIM = 64
N_HEADS = D_MODEL // HEAD_DIM
N_KV_HEADS = 4
GQA_GROUP = N_HEADS // N_KV_HEADS
ATTN_W = N_HEADS * HEAD_DIM
KV_W = N_KV_HEADS * HEAD_DIM
WINDOW = 128
BLOCK = 128
CHUNK = 128
GMLP_CH = 128
GMLP_GROUPS = 6
GMLP_W = GMLP_GROUPS * GMLP_CH
D_FF = 2816
CONV_W = 3
EPS = 1e-5
NEG = -1e30
IN_SIZES = (ATTN_W, KV_W, KV_W, GMLP_W, GMLP_W, D_MODEL, D_MODEL)
IN_COLS = ATTN_W + 2 * KV_W + 2 * GMLP_W + 2 * D_MODEL

kernel_name = 'hybrid_swa_sink_gmlp_convffn_step'


def rmsnorm(x, g):
    xf = x.astype(jnp.float32)
    y = xf * lax.rsqrt(jnp.mean(xf * xf, axis=-1, keepdims=True) + EPS)
    return (y * g.astype(jnp.float32)).astype(x.dtype)


def layernorm(x, g, b):
    xf = x.astype(jnp.float32)
    mu = jnp.mean(xf, axis=-1, keepdims=True)
    var = jnp.mean(jnp.square(xf - mu), axis=-1, keepdims=True)
    y = (xf - mu) * lax.rsqrt(var + EPS)
    return (y * g.astype(jnp.float32) + b.astype(jnp.float32)).astype(x.dtype)


def split_in(z):
    idx = [int(i) for i in np.cumsum(IN_SIZES)[:-1]]
    return jnp.split(z, idx, axis=-1)


def sink_attention(q, k, v, mask, sinks):
    s = jnp.einsum('...qhgd,...khd->...hgqk', q.astype(jnp.float32), k.astype(jnp.float32))
    s = jnp.where(mask, s * (HEAD_DIM ** -0.5), NEG)
    sink = jnp.broadcast_to(sinks.astype(jnp.float32).reshape(N_KV_HEADS, GQA_GROUP, 1, 1),
                            s.shape[:-1] + (1,))
    m = jnp.maximum(jnp.max(s, axis=-1, keepdims=True), sink)
    p = jnp.exp(s - m)
    denom = jnp.sum(p, axis=-1, keepdims=True) + jnp.exp(sink - m)
    o = jnp.einsum('...hgqk,...khd->...qhgd', p / denom, v.astype(jnp.float32))
    return o.astype(v.dtype)


def attn_prompt(q, k, v, sinks):
    B, T = q.shape[:2]
    nb = T // BLOCK
    qb = q.reshape(B, nb, BLOCK, N_KV_HEADS, GQA_GROUP, HEAD_DIM)
    kb = k.reshape(B, nb, BLOCK, N_KV_HEADS, HEAD_DIM)
    vb = v.reshape(B, nb, BLOCK, N_KV_HEADS, HEAD_DIM)
    pad = ((0, 0), (1, 0), (0, 0), (0, 0), (0, 0))
    kk = jnp.concatenate([jnp.pad(kb, pad)[:, :-1], kb], axis=2)
    vv = jnp.concatenate([jnp.pad(vb, pad)[:, :-1], vb], axis=2)
    diff = jnp.arange(BLOCK)[:, None] + BLOCK - jnp.arange(2 * BLOCK)[None, :]
    band = (diff >= 0) & (diff < WINDOW)
    valid = (jnp.arange(nb)[:, None] > 0) | (jnp.arange(2 * BLOCK)[None, :] >= BLOCK)
    mask = band[None] & valid[:, None, :]
    o = sink_attention(qb, kk, vv, mask[:, None, None], sinks)
    return o.reshape(B, T, ATTN_W), k[:, T - WINDOW:], v[:, T - WINDOW:]


def attn_sample(q, k, v, buf_k, buf_v, sinks):
    B, T = q.shape[:2]
    L = buf_k.shape[1]
    kk = jnp.concatenate([buf_k, k], axis=1)
    vv = jnp.concatenate([buf_v, v], axis=1)
    diff = jnp.arange(T)[:, None] + L - jnp.arange(L + T)[None, :]
    mask = (diff >= 0) & (diff < WINDOW)
    o = sink_attention(q, kk, vv, mask, sinks)
    return o.reshape(B, T, ATTN_W), kk[:, T:], vv[:, T:]


def spatial_gating(u, vn, ws, bs):
    B, T = u.shape[:2]
    C = min(T, CHUNK)
    nc = T // C
    w = jnp.where(jnp.tril(jnp.ones((C, C), dtype=bool)), ws[:, :C, :C], 0.0)
    vb = vn.reshape(B, nc, C, GMLP_GROUPS, GMLP_CH)
    mix = jnp.einsum('gts,bnsgc->bntgc', w, vb) + bs[:, :C].T[:, :, None]
    return u * mix.reshape(B, T, GMLP_W)


def conv_ffn(h, w_up, conv_w, conv_b, w_down, prev):
    z = h @ w_up
    T = z.shape[1]
    zp = jnp.concatenate([prev, z], axis=1)
    c = conv_b + sum(zp[:, j:j + T] * conv_w[j] for j in range(CONV_W))
    a, b = jnp.split(c, 2, axis=-1)
    return (jax.nn.gelu(a) * b) @ w_down, zp[:, T:]


def layer(x, norm_mix, w_in, sinks, ln_g, ln_b, ws, bs, w_pa, w_pb, w_out,
          norm_ffn, w_up, conv_w, conv_b, w_down, win_k, win_v, conv_prev):
    B, T, _ = x.shape
    h = rmsnorm(x, norm_mix)
    q, k, v, gu, gv, ga, gb = split_in(h @ w_in)
    q = q.reshape(B, T, N_KV_HEADS, GQA_GROUP, HEAD_DIM)
    k = k.reshape(B, T, N_KV_HEADS, HEAD_DIM)
    v = v.reshape(B, T, N_KV_HEADS, HEAD_DIM)
    if win_k is None:
        o, nk, nv = attn_prompt(q, k, v, sinks)
    else:
        o, nk, nv = attn_sample(q, k, v, win_k, win_v, sinks)
    vn = layernorm(jax.nn.gelu(gv), ln_g, ln_b)
    s = spatial_gating(jax.nn.gelu(gu), vn, ws, bs)
    merged = jax.nn.sigmoid(ga) * (o @ w_pa) + jax.nn.sigmoid(gb) * (s @ w_pb)
    x = x + merged @ w_out
    f, nconv = conv_ffn(rmsnorm(x, norm_ffn), w_up, conv_w, conv_b, w_down, conv_prev)
    return x + f, nk, nv, nconv, vn


def setup_inputs(seed: int = 0) -> dict:
    key = jax.random.key(seed)
    ks = jax.random.split(key, 24)
    f32 = jnp.float32

    def nrm(k, shape, scale):
        return jax.random.normal(k, shape, f32) * scale

    win_buf = min(WINDOW, PAST_LEN)
    return {
        'x_prompt': nrm(ks[0], (BATCH, SEQ, D_MODEL), 1.0),
        'x_sample': nrm(ks[1], (DEC_BATCH, DEC_SEQ, D_MODEL), 1.0),
        'cache_win_k': nrm(ks[2], (DEPTH, DEC_BATCH, win_buf, N_KV_HEADS, HEAD_DIM), 1.0),
        'cache_win_v': nrm(ks[3], (DEPTH, DEC_BATCH, win_buf, N_KV_HEADS, HEAD_DIM), 1.0),
        'state_conv': nrm(ks[4], (DEPTH, DEC_BATCH, CONV_W - 1, 2 * D_FF), 1.0),
        'norm_mix': 1.0 + nrm(ks[5], (DEPTH, D_MODEL), 0.02),
        'w_in': nrm(ks[6], (DEPTH, D_MODEL, IN_COLS), D_MODEL ** -0.5),
        'sinks': nrm(ks[7], (DEPTH, N_HEADS), 1.0),
        'gmlp_ln_g': 1.0 + nrm(ks[8], (DEPTH, GMLP_W), 0.02),
        'gmlp_ln_b': nrm(ks[9], (DEPTH, GMLP_W), 0.02),
        'gmlp_ws': nrm(ks[10], (DEPTH, GMLP_GROUPS, CHUNK, CHUNK), CHUNK ** -0.5),
        'gmlp_bs': 1.0 + nrm(ks[11], (DEPTH, GMLP_GROUPS, CHUNK), 0.05),
        'w_branch_attn': nrm(ks[12], (DEPTH, ATTN_W, D_MODEL), ATTN_W ** -0.5),
        'w_branch_gmlp': nrm(ks[13], (DEPTH, GMLP_W, D_MODEL), GMLP_W ** -0.5),
        'w_out': nrm(ks[14], (DEPTH, D_MODEL, D_MODEL), D_MODEL ** -0.5),
        'norm_ffn': 1.0 + nrm(ks[15], (DEPTH, D_MODEL), 0.02),
        'w_up': nrm(ks[16], (DEPTH, D_MODEL, 2 * D_FF), D_MODEL ** -0.5),
        'conv_w': nrm(ks[17], (DEPTH, CONV_W, 2 * D_FF), CONV_W ** -0.5),
        'conv_b': nrm(ks[18], (DEPTH, 2 * D_FF), 0.02),
        'w_down': nrm(ks[19], (DEPTH, D_FF, D_MODEL), D_FF ** -0.5),
        'norm_final': 1.0 + nrm(ks[20], (D_MODEL,), 0.02),
    }


def reference(x_prompt, x_sample, cache_win_k, cache_win_v, state_conv,
              norm_mix, w_in, sinks, gmlp_ln_g, gmlp_ln_b, gmlp_ws, gmlp_bs,
              w_branch_attn, w_branch_gmlp, w_out, norm_ffn, w_up, conv_w, conv_b,
              w_down, norm_final):
    xp, xs = x_prompt, x_sample
    conv0 = jnp.zeros((x_prompt.shape[0], CONV_W - 1, 2 * D_FF), x_prompt.dtype)
    kp_l, vp_l, cp_l, ks_l, vs_l, cs_l, gv_l = [], [], [], [], [], [], []
    for l in range(DEPTH):
        params = (norm_mix[l], w_in[l], sinks[l], gmlp_ln_g[l], gmlp_ln_b[l], gmlp_ws[l],
                  gmlp_bs[l], w_branch_attn[l], w_branch_gmlp[l], w_out[l], norm_ffn[l],
                  w_up[l], conv_w[l], conv_b[l], w_down[l])
        xp, kp, vp, cp, _ = layer(xp, *params, None, None, conv0)
        xs, kss, vss, css, gvs = layer(xs, *params, cache_win_k[l], cache_win_v[l], state_conv[l])
        kp_l.append(kp); vp_l.append(vp); cp_l.append(cp)
        ks_l.append(kss); vs_l.append(vss); cs_l.append(css); gv_l.append(gvs)
    y_prompt = rmsnorm(xp, norm_final)
    y_sample = rmsnorm(xs, norm_final)
    return (y_prompt, y_sample, jnp.stack(kp_l), jnp.stack(vp_l), jnp.stack(cp_l),
            jnp.stack(ks_l), jnp.stack(vs_l), jnp.stack(cs_l), jnp.stack(gv_l))
```

```python
import contextlib
import numpy as np
import concourse.bass as bass
import concourse.mybir as mybir
from concourse.bass_utils import run_bass_kernel_spmd

F32 = mybir.dt.float32
BF16 = mybir.dt.bfloat16
AF = mybir.ActivationFunctionType
ALU = mybir.AluOpType

NCORES = 8
D = 1024
NT = 35
TE = 4096
NV = 200
NSLOT = 6
PREF = 5
ENGS = ("pe", "act", "dve", "pool", "sp")
NDMASEM = 24
QA = [0, 1, 2, 3, 8, 9, 10, 11]
QB = [4, 5, 6, 7, 12, 13, 14, 15]


class Op:
    __slots__ = ("eng", "fn", "deps", "inc", "semval", "dma", "dsem", "dval", "prev_same_sem")

    def __init__(self, eng, fn, dma):
        self.eng = eng
        self.fn = fn
        self.deps = []
        self.inc = False
        self.semval = 0
        self.dma = dma
        self.dsem = None
        self.dval = 0
        self.prev_same_sem = None


class Sched:
    def __init__(self, nc):
        self.nc = nc
        self.ops = {e: [] for e in ENGS}
        self.all_ops = []
        self.last_w = {}
        self.readers = {}
        self.ndma = 0
        self.dma_last = [None] * NDMASEM
        self.dma_cnt = [0] * NDMASEM

    def op(self, eng, fn, reads=(), writes=(), dma=False):
        o = Op(eng, fn, dma)
        deps = {}
        for b in reads:
            w = self.last_w.get(b)
            if w is not None:
                deps[id(w)] = w
        for b in writes:
            w = self.last_w.get(b)
            if w is not None:
                deps[id(w)] = w
            for r in self.readers.get(b, ()):
                deps[id(r)] = r
        for b in writes:
            self.last_w[b] = o
            self.readers[b] = []
        for b in reads:
            if b not in writes:
                self.readers.setdefault(b, []).append(o)
        o.deps = [d for d in deps.values() if d is not o]
        if dma:
            s = self.ndma % NDMASEM
            self.ndma += 1
            o.dsem = s
            self.dma_cnt[s] += 1
            o.dval = 16 * self.dma_cnt[s]
            o.prev_same_sem = self.dma_last[s]
            self.dma_last[s] = o
        self.ops[eng].append(o)
        self.all_ops.append(o)
        return o

    def emit(self):
        nc = self.nc
        for o in self.all_ops:
            for d in o.deps:
                if not d.dma:
                    if d.eng == "pe" and o.eng == "pe" and not o.dma:
                        continue
                    d.inc = True
        for e in ENGS:
            c = 0
            for o in self.ops[e]:
                if o.inc and not o.dma:
                    c += 1
                    o.semval = c
        with contextlib.ExitStack() as st:
            esem = {e: st.enter_context(nc.semaphore("s_" + e)) for e in ENGS}
            dsem = [st.enter_context(nc.semaphore("d_%d" % i)) for i in range(NDMASEM)]
            block = st.enter_context(nc.Block())

            def run(e, engobj):
                waited = {}

                def wait(key, sem, val):
                    if waited.get(key, 0) >= val:
                        return
                    waited[key] = val
                    engobj.wait_ge(sem, val)

                for o in self.ops[e]:
                    for d in o.deps:
                        if d.dma:
                            wait(("d", d.dsem), dsem[d.dsem], d.dval)
                        else:
                            if d.eng == "pe" and e == "pe" and not o.dma:
                                continue
                            wait(("e", d.eng), esem[d.eng], d.semval)
                    if o.dma and o.prev_same_sem is not None:
                        p = o.prev_same_sem
                        wait(("d", p.dsem), dsem[p.dsem], p.dval)
                    ins = o.fn(engobj)
                    if o.dma:
                        ins.then_inc(dsem[o.dsem], 16)
                    elif o.inc:
                        ins.then_inc(esem[e], 1)
                if e == "sp":
                    for s in range(NDMASEM):
                        if self.dma_cnt[s]:
                            engobj.wait_ge(dsem[s], 16 * self.dma_cnt[s])

            block.tensor(lambda eng: run("pe", eng))
            block.scalar(lambda eng: run("act", eng))
            block.vector(lambda eng: run("dve", eng))
            block.gpsimd(lambda eng: run("pool", eng))
            block.sync(lambda eng: run("sp", eng))


def build_program(nown, depth, nlayers_run=None):
    L = depth
    halo = ((2 * L + 3) // 4) * 4
    NB = nown + halo
    NG = NB // 4
    G0 = halo // 4
    TP = NB * 128
    TO = nown * 128
    nc = bass.Bass("TRN2", target_bir_lowering=False)

    def din(name, shape, dt=F32):
        return nc.dram_tensor(name, list(shape), dt, kind="ExternalInput").ap()

    def dout(name, shape, dt=F32):
        return nc.dram_tensor(name, list(shape), dt, kind="ExternalOutput").ap()

    xT_d = din("xT", [D, TP])
    xsT_d = din("xsT", [D, 128])
    wt_d = din("wt", [L, NT, 128, TE])
    vecs_d = din("vecs", [128, L, NV])
    gfin_d = din("gfin", [128, 8])
    lnp_d = din("lnp", [L, 2, 128, 768])
    wsT_d = din("wsT", [L, 128, 6, 128])
    wsS_d = din("wsS", [L, 128, 6, 128])
    bsb_d = din("bsb", [L, 128, 6, 128])
    bsS_d = din("bsS", [L, 128, 6, 128])
    m2_d = din("m2", [128, 256])
    m2f_d = din("m2f", [128, 256])
    tri_d = din("tri", [128, 128])
    bdtri_d = din("bdtri", [128, 128])
    sms_d = din("sms", [128, 128])
    ident_d = din("ident", [128, 128])
    flag_d = din("flag", [128, 1])
    ck_d = din("ck", [L, 16, 128, 256])
    cv_d = din("cv", [L, 16, 128, 256])
    ckT_d = din("ckT", [L, 128, 16, 2, 128])
    scT_d = din("scT", [L, 128, 44, 32])

    yT_d = dout("yT", [D, TO])
    ysT_d = dout("ysT", [D, 128])
    kpo_d = dout("kpo", [L, 256, 128])
    vpo_d = dout("vpo", [L, 128, 256])
    zpo_d = dout("zpo", [L, 128, 44, 2])
    kso_d = dout("kso", [L, 16, 128, 256])
    vso_d = dout("vso", [L, 16, 128, 256])
    zso_d = dout("zso", [L, 128, 44, 32])
    gvo_d = dout("gvo", [L, 128, 768])

    wc_d = nc.dram_tensor("wcache", [L, NT, 128, TE], BF16, kind="Internal").ap()

    S = Sched(nc)
    st = contextlib.ExitStack()
    with st:
        def sb(name, shape, dt=F32):
            return st.enter_context(nc.sbuf_tensor(name, list(shape), dt))

        def psum(name):
            return st.enter_context(nc.psum_tensor(name, [128, 512], F32))

        xT = sb("xT_sb", [128, 8, 512])
        hT = sb("hT", [128, 8, 512], BF16)
        qT = sb("qT", [128, 8, 512], BF16)
        kT = sb("kT", [128, 2, 640], BF16)
        vb = sb("vb", [128, 5, 256], BF16)
        guT = sb("guT", [128, 6, 512], BF16)
        scrA = sb("scrA", [128, 3072])
        gvf = scrA[:, :].rearrange("p (b f) -> p b f", f=768)
        vnb = sb("vnb", [128, 4, 768], BF16)
        oT = sb("oT", [128, 8, 512], BF16)
        sT = guT
        mT = qT
        gT = sb("gT", [128, 22, 512], BF16)
        rstd = sb("rstd", [128, 512])
        lnt = sb("lnt", [128, 512])
        wslots = [sb("wslot%d" % i, [128, TE], BF16) for i in range(NSLOT)]
        pT = [[sb("pT%d%d" % (h, p), [128, 256], BF16) for p in range(4)] for h in range(2)]
        rden = [sb("rden%d" % i, [128, 128]) for i in range(4)]
        tg = [sb("tg%d" % i, [128, 512]) for i in range(2)]
        m1 = sb("m1", [128, 512])
        m2t = sb("m2t", [128, 512])
        zb = [sb("zb%d" % i, [128, 514]) for i in range(2)]
        c0 = [scrA[:, i * 512:(i + 1) * 512] for i in range(6)]
        zer = sb("zer", [128, 8])
        mixt = [sb("mixt%d" % i, [128, 128], BF16) for i in range(2)]
        lnst = sb("lnst", [128, 16])
        xc = sb("xc", [128, 768], BF16)
        ones = sb("ones", [128, 128], BF16)
        epsc = sb("epsc", [128, 1])
        m2 = sb("m2_sb", [128, 256], BF16)
        m2f = sb("m2f_sb", [128, 256], BF16)
        tri = sb("tri_sb", [128, 128])
        bdtri = sb("bdtri_sb", [128, 128])
        sms = sb("sms_sb", [128, 128], BF16)
        ident = sb("ident_sb", [128, 128], BF16)
        lnd = rden
        flag = sb("flag_sb", [128, 1])
        vecs = sb("vecs_sb", [128, L, NV])
        esk = sb("esk", [128, L, 8])
        eskx = sb("eskx", [128, L, 64])
        gfin = sb("gfin_sb", [128, 8])
        lng = sb("lng", [128, 768])
        lnb = sb("lnb", [128, 768])
        wsf = sb("wsf", [128, 6, 128])
        wsb = sb("wsb", [128, 6, 128], BF16)
        bsb = sb("bsb_sb", [128, 6, 128])
        kprev = sb("kprev", [128, L, 2, 128], BF16)
        vprev = sb("vprev", [128, L, 256], BF16)
        ztail = sb("ztail", [128, L, 44, 2])
        kpo_t = sb("kpo_t", [128, 2, 128])
        vpo_t = sb("vpo_t", [128, 256])
        ckTs = sb("ckTs", [128, 8, 2, 128], BF16)
        cvs = sb("cvs", [128, 8, 256], BF16)
        vseq = gT[0:8, 0:16, 128:384]
        sconv = sb("sconv", [128, 44, 32])
        zso_t = sconv
        kso_t = sb("kso_t", [128, 256])
        vso_t = sb("vso_t", [128, 256])

        ps = [psum("ps%d" % i) for i in range(8)]
        PROJ = [0, 1, 2, 7]
        proj_ctr = [0]

        def next_proj():
            i = PROJ[proj_ctr[0] % len(PROJ)]
            proj_ctr[0] += 1
            return ps[i], "ps%d" % i

        stream = []
        converted = set()
        emitted = [0]

        def tile_slot(i):
            return wslots[i % NSLOT], "wslot%d" % (i % NSLOT)

        def need(i):
            while emitted[0] <= min(i + NSLOT - 1, len(stream) - 1):
                j = emitted[0]
                l, t = stream[j]
                if (l, t) not in converted:
                    converted.add((l, t))
                    S.op("pool", lambda e, l=l, t=t: e.dma_start(out=wc_d[l, t], in_=wt_d[l, t], max_dma_last_dim=2048 * 4),
                         reads=[], writes=["wc%d_%d" % (l, t)], dma=True)
                slot, sname = tile_slot(j)
                S.op("sp", lambda e, l=l, t=t, slot=slot: e.dma_start(out=slot[:], in_=wc_d[l, t]),
                     reads=["wc%d_%d" % (l, t)], writes=[sname], dma=True)
                emitted[0] += 1

        passes = []
        Lr = L if nlayers_run is None else nlayers_run
        for g in range(NG):
            for l in range(Lr):
                b0 = max(4 * g, 2 * l)
                if b0 <= 4 * g + 3:
                    passes.append(("P", g, l, (b0 - 4 * g) * 128, (4 * g + 4 - b0) * 128))
        for l in range(Lr):
            passes.append(("S", NG, l, 0, 128))
        import os as _os
        F_PEMASK = _os.environ.get("F1", "1") == "1"
        F_ACTRD = _os.environ.get("F2", "1") == "1"
        F_LNRE = _os.environ.get("F3", "1") == "1"
        _np = int(_os.environ.get("KPASSES", "0"))
        _skipP = int(_os.environ.get("KSKIP", "0"))
        if _np:
            passes = passes[_skipP:_skipP + _np]
        _stage_lim = int(_os.environ.get("KSTAGE", "0"))
        for p in passes:
            for t in range(NT):
                stream.append((p[2], t))
        sidx = [0]

        def next_tiles(n):
            i0 = sidx[0]
            sidx[0] += n
            need(i0)
            return [tile_slot(i0 + k) for k in range(n)]

        def next_tile():
            return next_tiles(1)[0]

        S.op("pool", lambda e: e.memset(ones[:], 1.0), writes=["ones"])
        S.op("pool", lambda e: e.memset(epsc[:], 1e-5), writes=["epsc"])
        S.op("pool", lambda e: e.memset(zer[:], 0.0), writes=["zer"])
        S.op("pool", lambda e: e.memset(kprev[:], 0.0), writes=["kprev"])
        S.op("pool", lambda e: e.memset(vprev[:], 0.0), writes=["vprev"])
        S.op("pool", lambda e: e.memset(ztail[:], 0.0), writes=["ztail%d" % l for l in range(L)])
        S.op("pool", lambda e: e.dma_start(out=m2[:], in_=m2_d[:, :]), writes=["m2"], dma=True)
        S.op("pool", lambda e: e.dma_start(out=m2f[:], in_=m2f_d[:, :]), writes=["m2f"], dma=True)
        S.op("pool", lambda e: e.dma_start(out=sms[:], in_=sms_d[:, :]), writes=["sms"], dma=True)
        S.op("pool", lambda e: e.dma_start(out=ident[:], in_=ident_d[:, :]), writes=["ident"], dma=True)
        S.op("sp", lambda e: e.dma_start(out=tri[:], in_=tri_d[:, :]), writes=["tri"], dma=True)
        S.op("sp", lambda e: e.dma_start(out=bdtri[:], in_=bdtri_d[:, :]), writes=["bdtri"], dma=True)
        S.op("sp", lambda e: e.dma_start(out=flag[:], in_=flag_d[:, :]), writes=["flag"], dma=True)
        S.op("sp", lambda e: e.dma_start(out=vecs[:], in_=vecs_d[:, :, :]), writes=["vecs"], dma=True)
        S.op("sp", lambda e: e.dma_start(out=gfin[:], in_=gfin_d[:, :]), writes=["gfin"], dma=True)
        for l in range(L):
            S.op("act", lambda e, l=l: e.activation(out=esk[:, l, :], in_=vecs[:, l, 192:200], func=AF.Exp),
                 reads=["vecs"], writes=["esk"])
            for c in range(8):
                S.op("dve", lambda e, l=l, c=c: e.tensor_scalar(out=eskx[:, l, c * 8:(c + 1) * 8], in0=zer[:, :], scalar1=esk[:, l, c:c + 1],
                                                                scalar2=None, op0=ALU.add),
                     reads=["esk", "zer"], writes=["eskx"])
        for l in range(Lr):
            S.op("sp", lambda e, l=l: e.dma_start(out=kso_d[l, :, 0:120, :], in_=ck_d[l, :, 8:128, :]), writes=["kso_c%d" % l], dma=True)
            S.op("sp", lambda e, l=l: e.dma_start(out=vso_d[l, :, 0:120, :], in_=cv_d[l, :, 8:128, :]), writes=["vso_c%d" % l], dma=True)

        def col(l, i):
            return vecs[:, l, i:i + 1]

        def rmsnorm(gcol_fn, w0, T, tag):
            for c in range(8):
                S.op("act", lambda e, c=c: e.activation(out=gT[:, c, w0:w0 + T], in_=xT[:, c, w0:w0 + T], func=AF.Square),
                     reads=["xT%d" % c], writes=["gT%d" % c])
            for c in range(8):
                S.op("pe", lambda e, c=c: e.matmul(ps[6][:, 0:T], lhsT=ones[:, :], rhs=gT[:, c, w0:w0 + T], start=(c == 0), stop=(c == 7)),
                     reads=["ones", "gT%d" % c], writes=["ps6"])
            S.op("act", lambda e: e.activation(out=lnt[:, 0:T], in_=ps[6][:, 0:T], func=AF.Ln, scale=1.0 / D, bias=epsc[:, 0:1]),
                 reads=["ps6", "epsc"], writes=["lnt"])
            S.op("act", lambda e: e.activation(out=rstd[:, 0:T], in_=lnt[:, 0:T], func=AF.Exp, scale=-0.5),
                 reads=["lnt"], writes=["rstd"])
            return

        def apply_norm(dst, dname, gcol_fn, w0, T, out_w0):
            for c in range(8):
                S.op("dve", lambda e, c=c: e.scalar_tensor_tensor(out=dst[:, c, out_w0:out_w0 + T], in0=xT[:, c, w0:w0 + T], scalar=gcol_fn(c),
                                                                  in1=rstd[:, 0:T], op0=ALU.mult, op1=ALU.mult),
                     reads=["xT%d" % c, "rstd", "vecs", "gfin"], writes=[dname])

        def mmA(slot, sname, ncols, col0, kch, rhs, rname, w0, T, pst, pname):
            for c in range(kch):
                S.op("pe", lambda e, c=c: e.matmul(pst[:, 0:T], lhsT=slot[:, c * ncols + col0: c * ncols + col0 + 128],
                                                   rhs=rhs[:, c, w0:w0 + T], start=(c == 0), stop=(c == kch - 1)),
                     reads=[sname, rname], writes=[pname])

        evac_ctr = [0]

        def evac_copy(dst_ap, src_ap, reads, writes):
            evac_ctr[0] += 1
            if evac_ctr[0] % 2:
                S.op("act", lambda e: e.activation(out=dst_ap, in_=src_ap, func=AF.Identity), reads=reads, writes=writes)
            else:
                S.op("dve", lambda e: e.tensor_copy(out=dst_ap, in_=src_ap), reads=reads, writes=writes)

        def run_pass(kind, g, l, w0, T):
            nblk = T // 128
            _base = sidx[0]

            def stop(st_):
                if _stage_lim and st_ >= _stage_lim:
                    sidx[0] = _base + NT
                    return True
                return False
            last_own = (kind == "P" and g == NG - 1) and not _os.environ.get("KNOOUT")
            isS = (kind == "S")
            S.op("sp", lambda e: e.dma_start(out=lng[:], in_=lnp_d[l, 0]), writes=["lng"], dma=True)
            S.op("sp", lambda e: e.dma_start(out=lnb[:], in_=lnp_d[l, 1]), writes=["lnb"], dma=True)
            S.op("sp", lambda e: e.dma_start(out=wsf[:], in_=(wsS_d if isS else wsT_d)[l]), writes=["wsf"], dma=True)
            S.op("sp", lambda e: e.dma_start(out=bsb[:], in_=(bsS_d if isS else bsb_d)[l]), writes=["bsb"], dma=True)
            msk = bdtri if isS else tri
            for gg in range(6):
                S.op("pool", lambda e, gg=gg: e.tensor_tensor(out=wsb[:, gg, :], in0=wsf[:, gg, :], in1=msk[:, :], op=ALU.mult),
                     reads=["wsf", "tri", "bdtri"], writes=["wsb"])
            if isS:
                S.op("sp", lambda e: e.dma_start(out=sconv[:], in_=scT_d[l]), writes=["sconv"], dma=True)
            else:
                S.op("pool", lambda e: e.tensor_copy(out=kT[:, :, w0:w0 + 128], in_=kprev[:, l, :, :]), reads=["kprev"], writes=["kT"])
                S.op("pool", lambda e: e.tensor_copy(out=vb[:, w0 // 128, :], in_=vprev[:, l, :]), reads=["vprev"], writes=["vb"])

            rmsnorm(None, w0, T, "n1")
            apply_norm(hT, "hT", lambda c: col(l, c), w0, T, w0)

            if stop(1):
                return
            for t in range(2):
                slot, sname = next_tile()
                for j in range(4):
                    pst, pname = next_proj()
                    mmA(slot, sname, 512, j * 128, 8, hT, "hT", w0, T, pst, pname)
                    evac_copy(qT[:, 4 * t + j, w0:w0 + T], pst[:, 0:T], [pname], ["qT"])
            slot, sname = next_tile()
            for j in range(2):
                pst, pname = next_proj()
                mmA(slot, sname, 512, j * 128, 8, hT, "hT", w0, T, pst, pname)
                evac_copy(kT[:, j, 128 + w0:128 + w0 + T], pst[:, 0:T], [pname], ["kT"])
                if last_own:
                    S.op("act", lambda e, j=j, pst=pst: e.activation(out=kpo_t[:, j, :], in_=pst[:, T - 128:T], func=AF.Identity),
                         reads=[pname, "kT"], writes=["kpo_t"])
            if last_own:
                for j in range(2):
                    S.op("sp", lambda e, j=j: e.dma_start(out=kpo_d[l, j * 128:(j + 1) * 128, :], in_=kpo_t[:, j, :]), reads=["kpo_t"], writes=["kpo%d_%d" % (l, j)], dma=True)
            for bi in range(nblk):
                pst, pname = next_proj()
                cb = w0 + bi * 128
                for c in range(8):
                    S.op("pe", lambda e, c=c, pst=pst, cb=cb, slot=slot: e.matmul(pst[:, 0:256], lhsT=hT[:, c, cb:cb + 128], rhs=slot[:, c * 512 + 256:c * 512 + 512],
                                                                       start=(c == 0), stop=(c == 7)), reads=[sname, "hT"], writes=[pname])
                if isS:
                    evac_copy(vso_t[:, :], pst[:, 0:256], [pname], ["vso_t"])
                else:
                    evac_copy(vb[:, cb // 128 + 1, :], pst[:, 0:256], [pname], ["vb"])
                    if last_own and bi == nblk - 1:
                        S.op("act", lambda e, pst=pst: e.activation(out=vpo_t[:, :], in_=pst[:, 0:256], func=AF.Identity), reads=[pname, "vb"], writes=["vpo_t"])
                        S.op("sp", lambda e: e.dma_start(out=vpo_d[l], in_=vpo_t[:]), reads=["vpo_t"], writes=["vpo%d" % l], dma=True)
            if isS:
                pst, pname = next_proj()
                for c in range(8):
                    S.op("pe", lambda e, c=c, pst=pst, slot=slot: e.matmul(pst[:, 0:256], lhsT=hT[:, c, 0:128], rhs=slot[:, c * 512:c * 512 + 256],
                                                                start=(c == 0), stop=(c == 7)), reads=[sname, "hT"], writes=[pname])
                evac_copy(kso_t[:, :], pst[:, 0:256], [pname], ["kso_t"])
                for s in range(16):
                    S.op("sp", lambda e, s=s: e.dma_start(out=kso_d[l, s, 120:128, :], in_=kso_t[8 * s:8 * s + 8, :]), reads=["kso_t"], writes=["kso_n%d_%d" % (l, s)], dma=True)
                    S.op("sp", lambda e, s=s: e.dma_start(out=vso_d[l, s, 120:128, :], in_=vso_t[8 * s:8 * s + 8, :]), reads=["vso_t"], writes=["vso_n%d_%d" % (l, s)], dma=True)
                for s2 in range(8):
                    pst, pname = next_proj()
                    for ss in range(2):
                        s = 2 * s2 + ss
                        for c in range(8):
                            S.op("pe", lambda e, c=c, pst=pst, s=s, ss=ss, slot=slot: e.matmul(pst[0:8, ss * 256:ss * 256 + 256], lhsT=hT[:, c, 8 * s:8 * s + 8],
                                                                                    rhs=slot[:, c * 512 + 256:c * 512 + 512], start=(c == 0), stop=(c == 7)),
                                 reads=[sname, "hT"], writes=[pname])
                    evac_copy(vseq[0:8, 2 * s2:2 * s2 + 2, :], pst[0:8, 0:512].rearrange("p (s d) -> p s d", d=256), [pname], ["vseq"])
            if stop(2):
                return
            (t3, n3), (t4, n4), (t5, n5) = next_tiles(3)
            for j in range(6):
                pst, pname = next_proj()
                if j < 4:
                    mmA(t3, n3, 512, j * 128, 8, hT, "hT", w0, T, pst, pname)
                else:
                    mmA(t4, n4, 512, (j - 4) * 128, 8, hT, "hT", w0, T, pst, pname)
                S.op("act", lambda e, j=j, pst=pst: e.activation(out=guT[:, j, w0:w0 + T], in_=pst[:, 0:T], func=AF.Gelu_apprx_tanh),
                     reads=[pname], writes=["guT"])
            for bi in range(nblk):
                cb = w0 + bi * 128
                pa_, na_ = next_proj()
                for c in range(8):
                    S.op("pe", lambda e, c=c, pa_=pa_, cb=cb: e.matmul(pa_[:, 0:256], lhsT=hT[:, c, cb:cb + 128], rhs=t4[:, c * 512 + 256:c * 512 + 512],
                                                                       start=(c == 0), stop=(c == 7)), reads=[n4, "hT"], writes=[na_])
                S.op("act", lambda e, bi=bi, pa_=pa_: e.activation(out=gvf[:, bi, 0:256], in_=pa_[:, 0:256], func=AF.Gelu_apprx_tanh),
                     reads=[na_], writes=["gvf%d" % bi])
                pb_, nb_ = next_proj()
                for c in range(8):
                    S.op("pe", lambda e, c=c, pb_=pb_, cb=cb: e.matmul(pb_[:, 0:512], lhsT=hT[:, c, cb:cb + 128], rhs=t5[:, c * 512:c * 512 + 512],
                                                                       start=(c == 0), stop=(c == 7)), reads=[n5, "hT"], writes=[nb_])
                S.op("act", lambda e, bi=bi, pb_=pb_: e.activation(out=gvf[:, bi, 256:768], in_=pb_[:, 0:512], func=AF.Gelu_apprx_tanh),
                     reads=[nb_], writes=["gvf%d" % bi])
            def ln_steps(bi):
                def s0():
                    S.op("dve", lambda e: e.reduce_sum(out=lnst[:, bi:bi + 1], in_=gvf[:, bi, :], axis=mybir.AxisListType.X),
                         reads=["gvf%d" % bi], writes=["lnst_a%d" % bi])

                def s1():
                    S.op("dve", lambda e: e.tensor_scalar(out=lnst[:, 4 + bi:5 + bi], in0=lnst[:, bi:bi + 1], scalar1=-1.0 / 768, scalar2=None, op0=ALU.mult),
                         reads=["lnst_a%d" % bi], writes=["lnst_b%d" % bi])
                    S.op("dve", lambda e: e.tensor_scalar(out=gvf[:, bi, :], in0=gvf[:, bi, :], scalar1=lnst[:, 4 + bi:5 + bi], scalar2=None, op0=ALU.add),
                         reads=["lnst_b%d" % bi, "gvf%d" % bi], writes=["gvf%d" % bi])

                def s2():
                    S.op("pool", lambda e: e.tensor_tensor(out=xc[:, :], in0=gvf[:, bi, :], in1=gvf[:, bi, :], op=ALU.mult),
                         reads=["gvf%d" % bi], writes=["xc"])

                def s3():
                    S.op("dve", lambda e: e.reduce_sum(out=lnst[:, 8 + bi:9 + bi], in_=xc[:, :], axis=mybir.AxisListType.X),
                         reads=["xc"], writes=["lnst_c%d" % bi])

                def s4():
                    S.op("act", lambda e: e.activation(out=lnst[:, 12 + bi:13 + bi], in_=lnst[:, 8 + bi:9 + bi], func=AF.Ln, scale=1.0 / 768, bias=epsc[:, 0:1]),
                         reads=["lnst_c%d" % bi, "epsc"], writes=["lnst_d%d" % bi])
                    S.op("act", lambda e: e.activation(out=lnst[:, 12 + bi:13 + bi], in_=lnst[:, 12 + bi:13 + bi], func=AF.Exp, scale=-0.5),
                         reads=["lnst_d%d" % bi], writes=["lnst_d%d" % bi])

                def s5():
                    S.op("dve", lambda e: e.scalar_tensor_tensor(out=gvf[:, bi, :], in0=gvf[:, bi, :], scalar=lnst[:, 12 + bi:13 + bi], in1=lng[:, :],
                                                                 op0=ALU.mult, op1=ALU.mult), reads=["lnst_d%d" % bi, "lng", "gvf%d" % bi], writes=["gvf%d" % bi])

                def s6():
                    if isS:
                        S.op("pool", lambda e: e.tensor_tensor(out=gvf[:, bi, :], in0=gvf[:, bi, :], in1=lnb[:, :], op=ALU.add),
                             reads=["lnb", "gvf%d" % bi], writes=["gvf%d" % bi])
                        S.op("sp", lambda e: e.dma_start(out=gvo_d[l], in_=gvf[:, 0, :]), reads=["gvf0", "c0_0", "c0_1"], writes=["gvo%d" % l], dma=True)
                        S.op("pool", lambda e: e.tensor_copy(out=vnb[:, bi, :], in_=gvf[:, bi, :]), reads=["gvf%d" % bi], writes=["vnb%d" % bi])
                    else:
                        S.op("pool", lambda e: e.tensor_tensor(out=vnb[:, bi, :], in0=gvf[:, bi, :], in1=lnb[:, :], op=ALU.add),
                             reads=["lnb", "gvf%d" % bi], writes=["vnb%d" % bi])
                return [s0, s1, s2, s3, s4, s5, s6]

            if stop(3):
                return
            def attn_unit_prompt(bi, hooks=()):
                blk = w0 // 128 + bi
                qc0 = w0 + bi * 128
                mk, mkn = (m2f, "m2f") if (g == G0 and blk == 0) else (m2, "m2")

                def ctx(c):
                    par = c % 4
                    ob = (par % 2) * 256
                    obank, obname = (ps[5], "psO%d" % par) if par < 2 else (ps[2], "ps2")
                    return par, ob, obank, obname

                def stA(c):
                    par, ob, obank, obname = ctx(c)
                    for h in range(2):
                        bank = (ps[3 + h] if par < 2 else ps[6 + h])
                        bname = "psS%d_%d" % (h, par) if par < 2 else ("ps6" if h == 0 else "ps7")
                        cbs = (par % 2) * 256
                        hp = slice(64 * h, 64 * h + 64)
                        S.op("pe", lambda e, bank=bank, cbs=cbs: e.matmul(bank[:, cbs:cbs + 256], lhsT=ident[:, :], rhs=mk[:, :], start=True, stop=False),
                             reads=["ident", mkn], writes=[bname])
                        S.op("pe", lambda e, bank=bank, hp=hp, c=c, cbs=cbs: e.matmul(bank[:, cbs:cbs + 128], lhsT=kT[hp, c // 4, qc0:qc0 + 128],
                                                                                      rhs=qT[hp, c, qc0:qc0 + 128], start=False, stop=False),
                             reads=["kT", "qT"], writes=[bname])
                        S.op("pe", lambda e, bank=bank, hp=hp, c=c, cbs=cbs: e.matmul(bank[:, cbs + 128:cbs + 256], lhsT=kT[hp, c // 4, qc0 + 128:qc0 + 256],
                                                                                      rhs=qT[hp, c, qc0:qc0 + 128], start=False, stop=True),
                             reads=["kT", "qT"], writes=[bname])
                        pt, ptn = pT[h][par], "pT%d%d" % (h, par)
                        S.op("act", lambda e, bank=bank, pt=pt, cbs=cbs: e.activation(out=pt[:, :], in_=bank[:, cbs:cbs + 256], func=AF.Exp, scale=0.125),
                             reads=[bname], writes=[ptn])

                def stB(c):
                    par, ob, obank, obname = ctx(c)
                    for h in range(2):
                        pt, ptn = pT[h][par], "pT%d%d" % (h, par)
                        gk = 2 * (c // 4) + h
                        op_ = obank[64 * h:64 * h + 64, ob:ob + 128]
                        dn_ = obank[64 * h:64 * h + 64, ob + 128:ob + 256]
                        S.op("pe", lambda e, op_=op_, pt=pt, gk=gk: e.matmul(op_, lhsT=vb[:, blk, gk * 64:gk * 64 + 64], rhs=pt[:, 0:128], start=True, stop=False),
                             reads=["vb", ptn], writes=[obname])
                        S.op("pe", lambda e, op_=op_, pt=pt, gk=gk: e.matmul(op_, lhsT=vb[:, blk + 1, gk * 64:gk * 64 + 64], rhs=pt[:, 128:256], start=False, stop=True),
                             reads=["vb", ptn], writes=[obname])
                        S.op("pe", lambda e, dn_=dn_, pt=pt: e.matmul(dn_, lhsT=ones[:, 0:64], rhs=pt[:, 0:128], start=True, stop=False),
                             reads=["ones", ptn], writes=[obname])
                        S.op("pe", lambda e, dn_=dn_, pt=pt: e.matmul(dn_, lhsT=ones[:, 0:64], rhs=pt[:, 128:256], start=False, stop=True),
                             reads=["ones", ptn], writes=[obname])
                    rd, rdn = rden[par], "rden%d" % par
                    S.op("dve", lambda e, rd=rd, c=c, ob=ob, obank=obank: e.tensor_scalar(out=rd[:, :], in0=obank[:, ob + 128:ob + 256], scalar1=esk[:, l, c:c + 1], scalar2=None, op0=ALU.add),
                         reads=[obname, "esk"], writes=[rdn])

                def stC(c):
                    par, ob, obank, obname = ctx(c)
                    rd, rdn = rden[par], "rden%d" % par
                    S.op("act", lambda e, rd=rd: e.activation(out=rd[:, :], in_=rd[:, :], func=AF.Ln), reads=[rdn], writes=[rdn])
                    S.op("act", lambda e, rd=rd: e.activation(out=rd[:, :], in_=rd[:, :], func=AF.Exp, scale=-1.0), reads=[rdn], writes=[rdn])

                def stD(c):
                    par, ob, obank, obname = ctx(c)
                    rd, rdn = rden[par], "rden%d" % par
                    S.op("dve", lambda e, rd=rd, c=c, ob=ob, obank=obank: e.tensor_tensor(out=oT[:, c, qc0:qc0 + 128], in0=obank[:, ob:ob + 128], in1=rd[:, :], op=ALU.mult),
                         reads=[obname, rdn], writes=["oT"])

                dB, dC, dD = [int(x) for x in _os.environ.get("KOFF", "2,3,5").split(",")]
                for t in range(8 + dD):
                    if t < 8:
                        stA(t)
                    if 0 <= t - dB < 8:
                        stB(t - dB)
                    if 0 <= t - dC < 8:
                        stC(t - dC)
                    if 0 <= t - dD < 8:
                        stD(t - dD)
                    if t < len(hooks):
                        hooks[t]()

            def attn_unit_sample(s):
                par = s % 2
                ob = par * 256
                q0 = 8 * s
                for h in range(2):
                    bank, bname = ps[3 + h], "psS%d_%d" % (h, par)
                    cbs = par * 256
                    hp = slice(64 * h, 64 * h + 64)
                    S.op("pe", lambda e, bank=bank: e.matmul(bank[:, cbs:cbs + 128], lhsT=ident[:, :], rhs=sms[:, :], start=True, stop=False),
                         reads=["ident", "sms"], writes=[bname])
                    for k2 in range(2):
                        qv = qT[hp, 4 * k2:4 * k2 + 4, q0:q0 + 8]
                        S.op("pe", lambda e, bank=bank, hp=hp, k2=k2, qv=qv: e.matmul(bank[:, cbs + k2 * 32:cbs + k2 * 32 + 32], lhsT=ckTs[hp, s % 8, k2, :],
                                                                                      rhs=qv, start=False, stop=False),
                             reads=["ckTs", "qT"], writes=[bname])
                        S.op("pe", lambda e, bank=bank, hp=hp, k2=k2, qv=qv: e.matmul(bank[0:8, cbs + 64 + k2 * 32:cbs + 64 + k2 * 32 + 32], lhsT=kT[hp, k2, 128 + q0:128 + q0 + 8],
                                                                                      rhs=qv, start=False, stop=(k2 == 1)),
                             reads=["kT", "qT"], writes=[bname])
                    pt, ptn = pT[h][par], "pT%d%d" % (h, par)
                    S.op("act", lambda e, bank=bank, pt=pt: e.activation(out=pt[:, 0:128], in_=bank[:, cbs:cbs + 128], func=AF.Exp, scale=0.125),
                         reads=[bname], writes=[ptn])
                pname = "psO%d" % par
                for h in range(2):
                    pt, ptn = pT[h][par], "pT%d%d" % (h, par)
                    for k2 in range(2):
                        gk = 2 * k2 + h
                        op_ = ps[5][64 * h:64 * h + 64, ob + k2 * 32:ob + k2 * 32 + 32]
                        dn_ = ps[5][64 * h:64 * h + 64, ob + 64 + k2 * 32:ob + 64 + k2 * 32 + 32]
                        S.op("pe", lambda e, op_=op_, pt=pt, gk=gk, k2=k2: e.matmul(op_, lhsT=cvs[:, s % 8, gk * 64:gk * 64 + 64], rhs=pt[:, k2 * 32:k2 * 32 + 32], start=True, stop=False),
                             reads=["cvs", ptn], writes=[pname])
                        S.op("pe", lambda e, op_=op_, pt=pt, gk=gk, k2=k2: e.matmul(op_, lhsT=vseq[0:8, s, gk * 64:gk * 64 + 64], rhs=pt[0:8, 64 + k2 * 32:64 + k2 * 32 + 32], start=False, stop=True),
                             reads=["vseq", ptn], writes=[pname])
                        S.op("pe", lambda e, dn_=dn_, pt=pt, k2=k2: e.matmul(dn_, lhsT=ones[:, 0:64], rhs=pt[:, k2 * 32:k2 * 32 + 32], start=True, stop=False),
                             reads=["ones", ptn], writes=[pname])
                        S.op("pe", lambda e, dn_=dn_, pt=pt, k2=k2: e.matmul(dn_, lhsT=ones[0:8, 0:64], rhs=pt[0:8, 64 + k2 * 32:64 + k2 * 32 + 32], start=False, stop=True),
                             reads=["ones", ptn], writes=[pname])
                rd, rdn = rden[par], "rden%d" % par
                S.op("dve", lambda e, rd=rd: e.tensor_tensor(out=rd[:, 0:64], in0=ps[5][:, ob + 64:ob + 128], in1=eskx[:, l, :], op=ALU.add),
                     reads=[pname, "eskx"], writes=[rdn])
                S.op("dve", lambda e, rd=rd: e.reciprocal(out=rd[:, 0:64], in_=rd[:, 0:64]), reads=[rdn], writes=[rdn])
                S.op("dve", lambda e, rd=rd: e.tensor_tensor(out=oT[:, :, q0:q0 + 8], in0=ps[5][:, ob:ob + 64].rearrange("p (c q) -> p c q", q=8),
                                                             in1=rd[:, 0:64].rearrange("p (c q) -> p c q", q=8), op=ALU.mult),
                     reads=[pname, rdn], writes=["oT"])

            if isS:
                for s in range(16):
                    if 2 <= s < 9:
                        ln_steps(0)[s - 2]()
                    if s % 8 == 0:
                        S.op("pool", lambda e, s=s: e.dma_start(out=ckTs[:], in_=ckT_d[l, :, s:s + 8]), writes=["ckTs"], dma=True)
                        S.op("pool", lambda e, s=s: e.dma_start(out=cvs[:], in_=cv_d[l, s:s + 8].rearrange("s k d -> k s d")), writes=["cvs"], dma=True)
                    attn_unit_sample(s)
            else:
                for bi in range(nblk):
                    hk = ln_steps(bi)
                    if _os.environ.get("KHOOK", "0") == "1":
                        attn_unit_prompt(bi, [(lambda: None), hk[0], hk[1], hk[2], hk[3], hk[4], hk[5], hk[6]])
                    else:
                        attn_unit_prompt(bi, [])
                        for f_ in hk:
                            f_()

            if stop(4):
                return
            for bi in range(nblk):
                cb = w0 + bi * 128
                for gg in range(6):
                    par = gg % 2
                    pbank, pname = next_proj()
                    pst = pbank[:, 0:128]
                    S.op("pe", lambda e, pst=pst, bi=bi, gg=gg: e.matmul(pst, lhsT=vnb[:, bi, gg * 128:gg * 128 + 128], rhs=wsb[:, gg, :], start=True, stop=True),
                         reads=["vnb%d" % bi, "wsb"], writes=[pname])
                    mt, mtn = mixt[par], "mixt%d" % par
                    S.op("dve", lambda e, pst=pst, mt=mt, gg=gg: e.tensor_tensor(out=mt[:, :], in0=pst, in1=bsb[:, gg, :], op=ALU.add),
                         reads=[pname, "bsb"], writes=[mtn])
                    S.op("pool", lambda e, mt=mt, gg=gg, cb=cb: e.tensor_tensor(out=sT[:, gg, cb:cb + 128], in0=mt[:, :], in1=guT[:, gg, cb:cb + 128], op=ALU.mult),
                         reads=[mtn, "guT"], writes=["guT"])

            if stop(5):
                return
            for hh in range(2):
                (tga, nga), (tgb, ngb), (tpa, npa), (tpb, npb) = next_tiles(4)
                GV = ["gvf0", "gvf1", "gvf2", "gvf3"]
                for j in range(4):
                    jj = 4 * hh + j
                    alt = (jj % 2 == 1)
                    tgA, tgAn, xa = (c0[0], "c0_0", GV) if alt else (tg[0], "tg0", [])
                    tgB, tgBn, xb = (c0[1], "c0_1", GV) if alt else (tg[1], "tg1", [])
                    mA, mAn, xc_ = (c0[2], "c0_2", GV) if alt else (m1, "m1", [])
                    mB, mBn, xd = (c0[3], "c0_3", GV) if alt else (m2t, "m2t", [])
                    p1, n1 = next_proj()
                    mmA(tga, nga, 512, j * 128, 8, hT, "hT", w0, T, p1, n1)
                    S.op("act", lambda e, p1=p1, tgA=tgA: e.activation(out=tgA[:, 0:T], in_=p1[:, 0:T], func=AF.Tanh, scale=0.5), reads=[n1], writes=[tgAn] + xa)
                    p2, n2 = next_proj()
                    mmA(tgb, ngb, 512, j * 128, 8, hT, "hT", w0, T, p2, n2)
                    S.op("act", lambda e, p2=p2, tgB=tgB: e.activation(out=tgB[:, 0:T], in_=p2[:, 0:T], func=AF.Tanh, scale=0.5), reads=[n2], writes=[tgBn] + xb)
                    p3, n3_ = next_proj()
                    mmA(tpa, npa, 512, j * 128, 8, oT, "oT", w0, T, p3, n3_)
                    S.op("dve", lambda e, p3=p3, tgA=tgA, mA=mA: e.scalar_tensor_tensor(out=mA[:, 0:T], in0=tgA[:, 0:T], scalar=1.0, in1=p3[:, 0:T], op0=ALU.add, op1=ALU.mult),
                         reads=[tgAn, n3_], writes=[mAn] + xc_)
                    p4, n4_ = next_proj()
                    mmA(tpb, npb, 512, j * 128, 6, sT, "guT", w0, T, p4, n4_)
                    S.op("dve", lambda e, p4=p4, tgB=tgB, mB=mB: e.scalar_tensor_tensor(out=mB[:, 0:T], in0=tgB[:, 0:T], scalar=1.0, in1=p4[:, 0:T], op0=ALU.add, op1=ALU.mult),
                         reads=[tgBn, n4_], writes=[mBn] + xd)
                    S.op("pool", lambda e, jj=jj, mA=mA, mB=mB: e.tensor_tensor(out=mT[:, jj, w0:w0 + T], in0=mA[:, 0:T], in1=mB[:, 0:T], op=ALU.add),
                         reads=[mAn, mBn], writes=["qT"])
            if stop(6):
                return
            for t in range(2):
                slot, sname = next_tile()
                for j in range(4):
                    jj = 4 * t + j
                    pst, pname = next_proj()
                    mmA(slot, sname, 512, j * 128, 8, mT, "qT", w0, T, pst, pname)
                    S.op("dve", lambda e, pst=pst, jj=jj: e.scalar_tensor_tensor(out=xT[:, jj, w0:w0 + T], in0=pst[:, 0:T], scalar=0.5, in1=xT[:, jj, w0:w0 + T],
                                                                                 op0=ALU.mult, op1=ALU.add), reads=[pname, "xT%d" % jj], writes=["xT%d" % jj])
            if stop(7):
                return
            rmsnorm(None, w0, T, "n2")
            apply_norm(hT, "hT", lambda c: col(l, 8 + c), w0, T, w0)

            if stop(8):
                return
            nseg = 16 if isS else 1
            sl = T // nseg
            deferred = []
            for i in range(11):
                slot, sname = next_tile()
                for q in range(4):
                    pc = 4 * i + q
                    r = pc % 6
                    pst, pname = next_proj()
                    mmA(slot, sname, 512, q * 128, 8, hT, "hT", w0, T, pst, pname)
                    z, zn = zb[pc % 2], "zb%d" % (pc % 2)
                    zv = z[:, 0:nseg * (sl + 2)].rearrange("p (s t) -> p s t", t=sl + 2)
                    pv = pst[:, 0:T].rearrange("p (s t) -> p s t", t=sl)
                    if isS:
                        S.op("pool", lambda e, zv=zv, pc=pc: e.tensor_copy(out=zv[:, :, 0:2], in_=sconv[:, pc, :].rearrange("p (s t) -> p s t", t=2)),
                             reads=["sconv"], writes=[zn + "h"])
                    else:
                        S.op("pool", lambda e, zv=zv, pc=pc: e.tensor_copy(out=zv[:, :, 0:2], in_=ztail[:, l, pc, :].rearrange("p (s t) -> p s t", t=2)),
                             reads=["ztail%d" % l], writes=[zn + "h"])
                    S.op("act", lambda e, zv=zv, pv=pv: e.activation(out=zv[:, :, 2:2 + sl], in_=pv, func=AF.Identity), reads=[pname], writes=[zn])
                    cc, cn = c0[r], "c0_%d" % r
                    S.op("act", lambda e, cc=cc, pst=pst, pc=pc: e.activation(out=cc[:, 0:T], in_=pst[:, 0:T], func=AF.Identity,
                                                                              scale=col(l, 16 + 88 + pc), bias=col(l, 16 + 132 + pc)),
                         reads=[pname, "vecs"], writes=[cn])
                    ccv = cc[:, 0:T].rearrange("p (s t) -> p s t", t=sl)
                    S.op("dve", lambda e, ccv=ccv, zv=zv, pc=pc: e.scalar_tensor_tensor(out=ccv, in0=zv[:, :, 1:1 + sl], scalar=col(l, 16 + 44 + pc), in1=ccv,
                                                                                        op0=ALU.mult, op1=ALU.add), reads=[zn, zn + "h", cn, "vecs"], writes=[cn])
                    if isS:
                        S.op("pool", lambda e, zv=zv, pc=pc: e.tensor_copy(out=zso_t[:, pc, :].rearrange("p (s t) -> p s t", t=2), in_=zv[:, :, sl:sl + 2]),
                             reads=[zn], writes=["sconv"])
                    else:
                        S.op("pool", lambda e, z=z, pc=pc: e.tensor_copy(out=ztail[:, l, pc, :], in_=z[:, T:T + 2]),
                             reads=[zn, zn + "h"], writes=["ztail%d" % l])
                    S.op("dve", lambda e, ccv=ccv, zv=zv, pc=pc: e.scalar_tensor_tensor(out=ccv, in0=zv[:, :, 0:sl], scalar=col(l, 16 + pc), in1=ccv,
                                                                                         op0=ALU.mult, op1=ALU.add), reads=[zn, zn + "h", cn, "vecs"], writes=[cn])
                    if q < 2:
                        deferred.append((pc + 2, (lambda cc=cc, cn=cn: S.op("act", lambda e: e.activation(out=cc[:, 0:T], in_=cc[:, 0:T], func=AF.Gelu_apprx_tanh),
                                                                              reads=[cn], writes=[cn]))))
                    else:
                        ja = 2 * i + q - 2
                        ra, rb = (pc - 2) % 6, pc % 6
                        deferred.append((pc + 2, (lambda ra=ra, rb=rb, ja=ja: S.op(
                            "pool", lambda e: e.tensor_tensor(out=gT[:, ja, w0:w0 + T], in0=c0[ra][:, 0:T], in1=c0[rb][:, 0:T], op=ALU.mult),
                            reads=["c0_%d" % ra, "c0_%d" % rb], writes=["gT%d" % ja]))))
                    while deferred and deferred[0][0] <= pc:
                        deferred.pop(0)[1]()
            while deferred:
                deferred.pop(0)[1]()
            if isS:
                S.op("sp", lambda e: e.dma_start(out=zso_d[l], in_=zso_t[:]), reads=["sconv"], writes=["zso%d" % l], dma=True)
            if stop(9):
                return
            for j in range(8):
                slot, sname = next_tile()
                pst, pname = next_proj()
                for c in range(22):
                    S.op("pe", lambda e, c=c, pst=pst, slot=slot: e.matmul(pst[:, 0:T], lhsT=slot[:, c * 128:c * 128 + 128], rhs=gT[:, c, w0:w0 + T],
                                                                           start=(c == 0), stop=(c == 21)), reads=[sname, "gT%d" % c], writes=[pname])
                S.op("dve", lambda e, pst=pst, j=j: e.tensor_tensor(out=xT[:, j, w0:w0 + T], in0=pst[:, 0:T], in1=xT[:, j, w0:w0 + T], op=ALU.add),
                     reads=[pname, "xT%d" % j], writes=["xT%d" % j])
            if not isS:
                S.op("pool", lambda e: e.tensor_copy(out=kprev[:, l, :, :], in_=kT[:, :, w0 + T:w0 + T + 128]), reads=["kT"], writes=["kprev"])
                S.op("pool", lambda e: e.tensor_copy(out=vprev[:, l, :], in_=vb[:, (w0 + T) // 128, :]), reads=["vb"], writes=["vprev"])
                if g == G0 - 1:
                    S.op("pool", lambda e: e.tensor_scalar(out=ztail[:, l, :, :], in0=ztail[:, l, :, :], scalar1=flag[:, 0:1], scalar2=None, op0=ALU.mult),
                         reads=["ztail%d" % l, "flag"], writes=["ztail%d" % l])
                if last_own:
                    S.op("sp", lambda e: e.dma_start(out=zpo_d[l].rearrange("p c t -> p (c t)"), in_=ztail[:, l, :, :].rearrange("p c t -> p (c t)")), reads=["ztail%d" % l], writes=["zpo%d" % l], dma=True)

        def finalize(kind, g):
            if _os.environ.get("KNOFIN"):
                return
            T = 128 if kind == "S" else 512
            rmsnorm(None, 0, T, "nf")
            o0 = (g - G0) * 512
            for c in range(8):
                yb, ybn = c0[c % 4], "c0_%d" % (c % 4)
                S.op("dve", lambda e, c=c, yb=yb: e.scalar_tensor_tensor(out=yb[:, 0:T], in0=xT[:, c, 0:T], scalar=gfin[:, c:c + 1], in1=rstd[:, 0:T],
                                                                         op0=ALU.mult, op1=ALU.mult), reads=["xT%d" % c, "rstd", "gfin"], writes=[ybn])
                if kind == "S":
                    S.op("sp", lambda e, c=c, yb=yb: e.dma_start(out=ysT_d[c * 128:(c + 1) * 128, :], in_=yb[:, 0:128]), reads=[ybn, "gvf0", "gvf1", "gvf2", "gvf3"], writes=["ysT%d" % c], dma=True)
                else:
                    S.op("sp", lambda e, c=c, yb=yb: e.dma_start(out=yT_d[c * 128:(c + 1) * 128, o0:o0 + 512], in_=yb[:, 0:512]), reads=[ybn, "gvf0", "gvf1", "gvf2", "gvf3"], writes=["yT%d_%d" % (g, c)], dma=True)

        cur = None
        for (kind, g, l, w0, T) in passes:
            if (kind, g) != cur:
                if cur is not None and (cur[0] == "S" or cur[1] >= G0):
                    finalize(*cur)
                cur = (kind, g)
                if kind == "S":
                    S.op("sp", lambda e: e.dma_start(out=xT[:, :, 0:128], in_=xsT_d.rearrange("(c p) t -> p c t", p=128)),
                         writes=["xT%d" % c for c in range(8)], dma=True)
                else:
                    S.op("sp", lambda e, g=g: e.dma_start(out=xT[:, :, :], in_=xT_d[:, g * 512:(g + 1) * 512].rearrange("(c p) t -> p c t", p=128)),
                         writes=["xT%d" % c for c in range(8)], dma=True)
            run_pass(kind, g, l, w0, T)
        if cur is not None and (cur[0] == "S" or cur[1] >= G0):
            finalize(*cur)
        S.emit()
    return nc


def _tileA(W, col0, ncols, kch):
    a = W[:, col0:col0 + ncols].reshape(kch, 128, ncols).transpose(1, 0, 2).reshape(128, kch * ncols)
    out = np.zeros((128, TE), np.float32)
    out[:, :kch * ncols] = a
    return out


def _prep_weights(w_in, w_pa, w_pb, w_out, w_up, w_down):
    L = w_in.shape[0]
    qperm = []
    for c in range(8):
        for hd in (QA[c], QB[c]):
            qperm.extend(range(hd * 64, hd * 64 + 64))
    qperm = np.array(qperm)
    upperm = []
    for i in range(11):
        for q in range(4):
            base = (2 * i + q) * 128 if q < 2 else 2816 + (2 * i + q - 2) * 128
            upperm.extend(range(base, base + 128))
    upperm = np.array(upperm)
    wt = np.zeros((L, NT, 128, TE), np.float32)
    for l in range(L):
        Wi = w_in[l]
        Wp = np.concatenate([Wi[:, 0:1024][:, qperm], Wi[:, 1024:]], axis=1)
        tl = []
        for t in range(6):
            tl.append(_tileA(Wp, t * 512, 512, 8))
        pa = w_pa[l][qperm, :]
        for hh in range(2):
            tl.append(_tileA(Wp, 3072 + hh * 512, 512, 8))
            tl.append(_tileA(Wp, 4096 + hh * 512, 512, 8))
            tl.append(_tileA(pa, hh * 512, 512, 8))
            tl.append(_tileA(w_pb[l], hh * 512, 512, 6))
        for t in range(2):
            tl.append(_tileA(w_out[l], t * 512, 512, 8))
        Wu = w_up[l][:, upperm]
        for i in range(11):
            tl.append(_tileA(Wu, i * 512, 512, 8))
        for j in range(8):
            tl.append(_tileA(w_down[l], j * 128, 128, 22))
        assert len(tl) == NT
        wt[l] = np.stack(tl)
    return wt, qperm, upperm


_PROG_CACHE = {}


def kernel(x_prompt, x_sample, cache_win_k, cache_win_v, state_conv, norm_mix, w_in, sinks, gmlp_ln_g, gmlp_ln_b,
           gmlp_ws, gmlp_bs, w_branch_attn, w_branch_gmlp, w_out, norm_ffn, w_up, conv_w, conv_b, w_down, norm_final,
           _nlayers_run=None):
    f = lambda a: np.ascontiguousarray(np.asarray(a, dtype=np.float32))
    x_prompt, x_sample, cache_win_k, cache_win_v, state_conv = map(f, (x_prompt, x_sample, cache_win_k, cache_win_v, state_conv))
    norm_mix, w_in, sinks, gmlp_ln_g, gmlp_ln_b, gmlp_ws, gmlp_bs = map(f, (norm_mix, w_in, sinks, gmlp_ln_g, gmlp_ln_b, gmlp_ws, gmlp_bs))
    w_branch_attn, w_branch_gmlp, w_out, norm_ffn, w_up, conv_w, conv_b, w_down, norm_final = map(
        f, (w_branch_attn, w_branch_gmlp, w_out, norm_ffn, w_up, conv_w, conv_b, w_down, norm_final))
    B, SEQ, _ = x_prompt.shape
    L = w_in.shape[0]
    NSEQ = x_sample.shape[0]
    cpb = NCORES // B
    nown = SEQ // cpb // 128
    halo = ((2 * L + 3) // 4) * 4
    NB = nown + halo
    TP = NB * 128
    spc = NSEQ // NCORES
    assert spc == 16 and x_sample.shape[1] == 8

    wt, qperm, upperm = _prep_weights(w_in, w_branch_attn, w_branch_gmlp, w_out, w_up, w_down)

    def colmajor(v, n):
        return v.reshape(n, 128).T

    vecs = np.zeros((128, L, NV), np.float32)
    for l in range(L):
        vecs[:, l, 0:8] = colmajor(norm_mix[l], 8)
        vecs[:, l, 8:16] = colmajor(norm_ffn[l], 8)
        for j in range(3):
            vecs[:, l, 16 + 44 * j:16 + 44 * (j + 1)] = colmajor(conv_w[l, j][upperm], 44)
        vecs[:, l, 148:192] = colmajor(conv_b[l][upperm], 44)
        for c in range(8):
            vecs[0:64, l, 192 + c] = sinks[l, QA[c]]
            vecs[64:128, l, 192 + c] = sinks[l, QB[c]]
    gfin = np.ascontiguousarray(colmajor(norm_final, 8))
    lnp = np.ascontiguousarray(np.broadcast_to(np.stack([gmlp_ln_g, gmlp_ln_b], axis=1)[:, :, None, :], (L, 2, 128, 768)))
    wsT = np.ascontiguousarray(gmlp_ws.transpose(0, 3, 1, 2))
    ws8 = gmlp_ws[:, :, :8, :8]
    wsS = np.ascontiguousarray(np.tile(ws8.transpose(0, 3, 1, 2), (1, 16, 1, 16)))
    bsb = np.ascontiguousarray(np.broadcast_to(gmlp_bs[:, None, :, :], (L, 128, 6, 128)))
    bsS = np.ascontiguousarray(np.broadcast_to(np.tile(gmlp_bs[:, :, :8], (1, 1, 16))[:, None, :, :], (L, 128, 6, 128)))
    kk = np.arange(128)[:, None]
    qq = np.arange(128)[None, :]
    NEGM = -30000.0
    m2 = np.where(np.concatenate([(kk > qq), (kk <= qq)], axis=1), 0.0, NEGM).astype(np.float32)
    m2z = m2.copy()
    m2z[:, :128] = NEGM
    tri = (kk <= qq).astype(np.float32)
    bdtri = ((kk // 8 == qq // 8) & (kk <= qq)).astype(np.float32)
    sms = np.zeros((128, 2, 8, 8), np.float32)
    k8 = np.arange(128)[:, None]
    q8 = np.arange(8)[None, :]
    sms[:, 0, :, :] = np.where(k8 > q8, 0.0, NEGM).astype(np.float32)[:, None, :]
    sms[:, 1, :, :] = np.where((k8 <= q8) & (k8 < 8), 0.0, NEGM).astype(np.float32)[:, None, :]
    sms = sms.reshape(128, 128)

    in_maps = []
    for core in range(NCORES):
        b = core // cpb
        r = core % cpb
        t0 = r * nown * 128 - halo * 128
        xs = np.zeros((TP, D), np.float32)
        lo = max(t0, 0)
        xs[lo - t0:] = x_prompt[b, lo:t0 + TP]
        s0 = core * spc
        ck = cache_win_k[:, s0:s0 + spc].reshape(L, spc, 128, 256)
        cv = cache_win_v[:, s0:s0 + spc].reshape(L, spc, 128, 256)
        ckT = np.ascontiguousarray(ck.reshape(L, spc, 128, 2, 128).transpose(0, 4, 1, 3, 2))
        sc = state_conv[:, s0:s0 + spc][:, :, :, upperm]
        scT = np.ascontiguousarray(sc.reshape(L, spc, 2, 44, 128).transpose(0, 4, 3, 1, 2).reshape(L, 128, 44, 32))
        first = (r == 0)
        in_maps.append({
            "xT": np.ascontiguousarray(xs.T),
            "xsT": np.ascontiguousarray(x_sample[s0:s0 + spc].reshape(spc * 8, D).T),
            "wt": wt, "vecs": vecs, "gfin": gfin, "lnp": lnp, "wsT": wsT, "wsS": wsS, "bsb": bsb, "bsS": bsS,
            "m2": m2, "m2f": (m2z if first else m2), "tri": tri, "bdtri": bdtri, "sms": sms, "ident": np.eye(128, dtype=np.float32),
            "flag": np.full((128, 1), 0.0 if first else 1.0, np.float32),
            "ck": np.ascontiguousarray(ck), "cv": np.ascontiguousarray(cv), "ckT": ckT, "scT": scT,
        })

    key = (nown, L, _nlayers_run)
    if key not in _PROG_CACHE:
        _PROG_CACHE[key] = build_program(nown, L, _nlayers_run)
    nc = _PROG_CACHE[key]
    res = run_bass_kernel_spmd(nc, in_maps, core_ids=list(range(NCORES)))
    R = res.results

    inv_up = np.argsort(upperm)
    y_prompt = np.zeros((B, SEQ, D), np.float32)
    y_sample = np.zeros((NSEQ, 8, D), np.float32)
    wkp = np.zeros((L, B, 128, 4, 64), np.float32)
    wvp = np.zeros((L, B, 128, 4, 64), np.float32)
    cvp = np.zeros((L, B, 2, 5632), np.float32)
    wks = np.zeros((L, NSEQ, 128, 4, 64), np.float32)
    wvs = np.zeros((L, NSEQ, 128, 4, 64), np.float32)
    cvs_ = np.zeros((L, NSEQ, 2, 5632), np.float32)
    gvs = np.zeros((L, NSEQ, 8, 768), np.float32)
    for core in range(NCORES):
        b = core // cpb
        r = core % cpb
        o = R[core]
        y_prompt[b, r * nown * 128:(r + 1) * nown * 128] = o["yT"].T
        s0 = core * spc
        y_sample[s0:s0 + spc] = o["ysT"].T.reshape(spc, 8, D)
        wks[:, s0:s0 + spc] = o["kso"].reshape(L, spc, 128, 4, 64)
        wvs[:, s0:s0 + spc] = o["vso"].reshape(L, spc, 128, 4, 64)
        zs = o["zso"].reshape(L, 128, 44, spc, 2).transpose(0, 3, 4, 2, 1).reshape(L, spc, 2, 5632)
        cvs_[:, s0:s0 + spc] = zs[:, :, :, inv_up]
        gvs[:, s0:s0 + spc] = o["gvo"].reshape(L, spc, 8, 768)
        if r == cpb - 1:
            wkp[:, b] = o["kpo"].transpose(0, 2, 1).reshape(L, 128, 4, 64)
            wvp[:, b] = o["vpo"].reshape(L, 128, 4, 64)
            zp = o["zpo"].transpose(0, 3, 2, 1).reshape(L, 2, 5632)
            cvp[:, b] = zp[:, :, inv_up]
    return (y_prompt, y_sample, wkp, wvp, cvp, wks, wvs, cvs_, gvs)
```

```python
import contextlib
import numpy as np
import concourse.bass as bass
import concourse.mybir as mybir
from concourse.bass_utils import run_bass_kernel_spmd

F32 = mybir.dt.float32
BF16 = mybir.dt.bfloat16
AF = mybir.ActivationFunctionType
ALU = mybir.AluOpType

NCORES = 8
D = 1024
NT = 35
TE = 4096
NV = 200
NSLOT = 6
PREF = 5
ENGS = ("pe", "act", "dve", "pool", "sp")
NDMASEM = 24
QA = [0, 1, 2, 3, 8, 9, 10, 11]
QB = [4, 5, 6, 7, 12, 13, 14, 15]


class Op:
    __slots__ = ("eng", "fn", "deps", "inc", "semval", "dma", "dsem", "dval", "prev_same_sem")

    def __init__(self, eng, fn, dma):
        self.eng = eng
        self.fn = fn
        self.deps = []
        self.inc = False
        self.semval = 0
        self.dma = dma
        self.dsem = None
        self.dval = 0
        self.prev_same_sem = None


class Sched:
    def __init__(self, nc):
        self.nc = nc
        self.ops = {e: [] for e in ENGS}
        self.all_ops = []
        self.last_w = {}
        self.readers = {}
        self.ndma = 0
        self.dma_last = [None] * NDMASEM
        self.dma_cnt = [0] * NDMASEM

    def op(self, eng, fn, reads=(), writes=(), dma=False):
        o = Op(eng, fn, dma)
        deps = {}
        for b in reads:
            w = self.last_w.get(b)
            if w is not None:
                deps[id(w)] = w
        for b in writes:
            w = self.last_w.get(b)
            if w is not None:
                deps[id(w)] = w
            for r in self.readers.get(b, ()):
                deps[id(r)] = r
        for b in writes:
            self.last_w[b] = o
            self.readers[b] = []
        for b in reads:
            if b not in writes:
                self.readers.setdefault(b, []).append(o)
        o.deps = [d for d in deps.values() if d is not o]
        if dma:
            s = self.ndma % NDMASEM
            self.ndma += 1
            o.dsem = s
            self.dma_cnt[s] += 1
            o.dval = 16 * self.dma_cnt[s]
            o.prev_same_sem = self.dma_last[s]
            self.dma_last[s] = o
        self.ops[eng].append(o)
        self.all_ops.append(o)
        return o

    def emit(self):
        nc = self.nc
        for o in self.all_ops:
            for d in o.deps:
                if not d.dma:
                    if d.eng == "pe" and o.eng == "pe" and not o.dma:
                        continue
                    d.inc = True
        for e in ENGS:
            c = 0
            for o in self.ops[e]:
                if o.inc and not o.dma:
                    c += 1
                    o.semval = c
        with contextlib.ExitStack() as st:
            esem = {e: st.enter_context(nc.semaphore("s_" + e)) for e in ENGS}
            dsem = [st.enter_context(nc.semaphore("d_%d" % i)) for i in range(NDMASEM)]
            block = st.enter_context(nc.Block())

            def run(e, engobj):
                waited = {}

                def wait(key, sem, val):
                    if waited.get(key, 0) >= val:
                        return
                    waited[key] = val
                    engobj.wait_ge(sem, val)

                for o in self.ops[e]:
                    for d in o.deps:
                        if d.dma:
                            wait(("d", d.dsem), dsem[d.dsem], d.dval)
                        else:
                            if d.eng == "pe" and e == "pe" and not o.dma:
                                continue
                            wait(("e", d.eng), esem[d.eng], d.semval)
                    if o.dma and o.prev_same_sem is not None:
                        p = o.prev_same_sem
                        wait(("d", p.dsem), dsem[p.dsem], p.dval)
                    ins = o.fn(engobj)
                    if o.dma:
                        ins.then_inc(dsem[o.dsem], 16)
                    elif o.inc:
                        ins.then_inc(esem[e], 1)
                if e == "sp":
                    for s in range(NDMASEM):
                        if self.dma_cnt[s]:
                            engobj.wait_ge(dsem[s], 16 * self.dma_cnt[s])

            block.tensor(lambda eng: run("pe", eng))
            block.scalar(lambda eng: run("act", eng))
            block.vector(lambda eng: run("dve", eng))
            block.gpsimd(lambda eng: run("pool", eng))
            block.sync(lambda eng: run("sp", eng))


def build_program(nown, depth, nlayers_run=None):
    L = depth
    halo = ((2 * L + 3) // 4) * 4
    NB = nown + halo
    NG = NB // 4
    G0 = halo // 4
    TP = NB * 128
    TO = nown * 128
    nc = bass.Bass("TRN2", target_bir_lowering=False)

    def din(name, shape, dt=F32):
        return nc.dram_tensor(name, list(shape), dt, kind="ExternalInput").ap()

    def dout(name, shape, dt=F32):
        return nc.dram_tensor(name, list(shape), dt, kind="ExternalOutput").ap()

    xT_d = din("xT", [D, TP])
    xsT_d = din("xsT", [D, 128])
    wt_d = din("wt", [L, NT, 128, TE])
    vecs_d = din("vecs", [128, L, NV])
    gfin_d = din("gfin", [128, 8])
    lnp_d = din("lnp", [L, 2, 128, 768])
    wsT_d = din("wsT", [L, 128, 6, 128])
    wsS_d = din("wsS", [L, 128, 6, 128])
    bsb_d = din("bsb", [L, 128, 6, 128])
    bsS_d = din("bsS", [L, 128, 6, 128])
    m2_d = din("m2", [128, 256])
    m2f_d = din("m2f", [128, 256])
    tri_d = din("tri", [128, 128])
    bdtri_d = din("bdtri", [128, 128])
    sms_d = din("sms", [128, 128])
    ident_d = din("ident", [128, 128])
    flag_d = din("flag", [128, 1])
    ck_d = din("ck", [L, 16, 128, 256])
    cv_d = din("cv", [L, 16, 128, 256])
    ckT_d = din("ckT", [L, 128, 16, 2, 128])
    scT_d = din("scT", [L, 128, 44, 32])

    yT_d = dout("yT", [D, TO])
    ysT_d = dout("ysT", [D, 128])
    kpo_d = dout("kpo", [L, 256, 128])
    vpo_d = dout("vpo", [L, 128, 256])
    zpo_d = dout("zpo", [L, 128, 44, 2])
    kso_d = dout("kso", [L, 16, 128, 256])
    vso_d = dout("vso", [L, 16, 128, 256])
    zso_d = dout("zso", [L, 128, 44, 32])
    gvo_d = dout("gvo", [L, 128, 768])

    wc_d = nc.dram_tensor("wcache", [L, NT, 128, TE], BF16, kind="Internal").ap()

    S = Sched(nc)
    st = contextlib.ExitStack()
    with st:
        def sb(name, shape, dt=F32):
            return st.enter_context(nc.sbuf_tensor(name, list(shape), dt))

        def psum(name):
            return st.enter_context(nc.psum_tensor(name, [128, 512], F32))

        xT = sb("xT_sb", [128, 8, 512])
        hT = sb("hT", [128, 8, 512], BF16)
        qT = sb("qT", [128, 8, 512], BF16)
        kT = sb("kT", [128, 2, 640], BF16)
        vb = sb("vb", [128, 5, 256], BF16)
        guT = sb("guT", [128, 6, 512], BF16)
        scrA = sb("scrA", [128, 3072])
        gvf = scrA[:, :].rearrange("p (b f) -> p b f", f=768)
        vnb = sb("vnb", [128, 4, 768], BF16)
        oT = sb("oT", [128, 8, 512], BF16)
        sT = guT
        mT = qT
        gT = sb("gT", [128, 22, 512], BF16)
        rstd = sb("rstd", [128, 512])
        lnt = sb("lnt", [128, 512])
        wslots = [sb("wslot%d" % i, [128, TE], BF16) for i in range(NSLOT)]
        pT = [[sb("pT%d%d" % (h, p), [128, 256], BF16) for p in range(4)] for h in range(2)]
        rden = [sb("rden%d" % i, [128, 128]) for i in range(4)]
        tg = [sb("tg%d" % i, [128, 512]) for i in range(2)]
        m1 = sb("m1", [128, 512])
        m2t = sb("m2t", [128, 512])
        zb = [sb("zb%d" % i, [128, 514]) for i in range(2)]
        c0 = [scrA[:, i * 512:(i + 1) * 512] for i in range(6)]
        zer = sb("zer", [128, 8])
        mixt = [sb("mixt%d" % i, [128, 128], BF16) for i in range(2)]
        lnst = sb("lnst", [128, 16])
        xc = sb("xc", [128, 768], BF16)
        ones = sb("ones", [128, 128], BF16)
        epsc = sb("epsc", [128, 1])
        m2 = sb("m2_sb", [128, 256], BF16)
        m2f = sb("m2f_sb", [128, 256], BF16)
        tri = sb("tri_sb", [128, 128])
        bdtri = sb("bdtri_sb", [128, 128])
        sms = sb("sms_sb", [128, 128], BF16)
        ident = sb("ident_sb", [128, 128], BF16)
        lnd = rden
        flag = sb("flag_sb", [128, 1])
        vecs = sb("vecs_sb", [128, L, NV])
        esk = sb("esk", [128, L, 8])
        eskx = sb("eskx", [128, L, 64])
        gfin = sb("gfin_sb", [128, 8])
        lng = sb("lng", [128, 768])
        lnb = sb("lnb", [128, 768])
        wsf = sb("wsf", [128, 6, 128])
        wsb = sb("wsb", [128, 6, 128], BF16)
        bsb = sb("bsb_sb", [128, 6, 128])
        kprev = sb("kprev", [128, L, 2, 128], BF16)
        vprev = sb("vprev", [128, L, 256], BF16)
        ztail = sb("ztail", [128, L, 44, 2])
        kpo_t = sb("kpo_t", [128, 2, 128])
        vpo_t = sb("vpo_t", [128, 256])
        ckTs = sb("ckTs", [128, 8, 2, 128], BF16)
        cvs = sb("cvs", [128, 8, 256], BF16)
        vseq = gT[0:8, 0:16, 128:384]
        sconv = sb("sconv", [128, 44, 32])
        zso_t = sconv
        kso_t = sb("kso_t", [128, 256])
        vso_t = sb("vso_t", [128, 256])

        ps = [psum("ps%d" % i) for i in range(8)]
        PROJ = [0, 1, 2, 7]
        proj_ctr = [0]

        def next_proj():
            i = PROJ[proj_ctr[0] % len(PROJ)]
            proj_ctr[0] += 1
            return ps[i], "ps%d" % i

        stream = []
        converted = set()
        emitted = [0]

        def tile_slot(i):
            return wslots[i % NSLOT], "wslot%d" % (i % NSLOT)

        def need(i):
            while emitted[0] <= min(i + NSLOT - 1, len(stream) - 1):
                j = emitted[0]
                l, t = stream[j]
                if (l, t) not in converted:
                    converted.add((l, t))
                    S.op("pool", lambda e, l=l, t=t: e.dma_start(out=wc_d[l, t], in_=wt_d[l, t], max_dma_last_dim=2048 * 4),
                         reads=[], writes=["wc%d_%d" % (l, t)], dma=True)
                slot, sname = tile_slot(j)
                S.op("sp", lambda e, l=l, t=t, slot=slot: e.dma_start(out=slot[:], in_=wc_d[l, t]),
                     reads=["wc%d_%d" % (l, t)], writes=[sname], dma=True)
                emitted[0] += 1

        passes = []
        Lr = L if nlayers_run is None else nlayers_run
        for g in range(NG):
            for l in range(Lr):
                b0 = max(4 * g, 2 * l)
                if b0 <= 4 * g + 3:
                    passes.append(("P", g, l, (b0 - 4 * g) * 128, (4 * g + 4 - b0) * 128))
        for l in range(Lr):
            passes.append(("S", NG, l, 0, 128))
        import os as _os
        F_PEMASK = _os.environ.get("F1", "1") == "1"
        F_ACTRD = _os.environ.get("F2", "1") == "1"
        F_LNRE = _os.environ.get("F3", "1") == "1"
        _np = int(_os.environ.get("KPASSES", "0"))
        _skipP = int(_os.environ.get("KSKIP", "0"))
        if _np:
            passes = passes[_skipP:_skipP + _np]
        _stage_lim = int(_os.environ.get("KSTAGE", "0"))
        for p in passes:
            for t in range(NT):
                stream.append((p[2], t))
        sidx = [0]

        def next_tiles(n):
            i0 = sidx[0]
            sidx[0] += n
            need(i0)
            return [tile_slot(i0 + k) for k in range(n)]

        def next_tile():
            return next_tiles(1)[0]

        S.op("pool", lambda e: e.memset(ones[:], 1.0), writes=["ones"])
        S.op("pool", lambda e: e.memset(epsc[:], 1e-5), writes=["epsc"])
        S.op("pool", lambda e: e.memset(zer[:], 0.0), writes=["zer"])
        S.op("pool", lambda e: e.memset(kprev[:], 0.0), writes=["kprev"])
        S.op("pool", lambda e: e.memset(vprev[:], 0.0), writes=["vprev"])
        S.op("pool", lambda e: e.memset(ztail[:], 0.0), writes=["ztail%d" % l for l in range(L)])
        S.op("pool", lambda e: e.dma_start(out=m2[:], in_=m2_d[:, :]), writes=["m2"], dma=True)
        S.op("pool", lambda e: e.dma_start(out=m2f[:], in_=m2f_d[:, :]), writes=["m2f"], dma=True)
        S.op("pool", lambda e: e.dma_start(out=sms[:], in_=sms_d[:, :]), writes=["sms"], dma=True)
        S.op("pool", lambda e: e.dma_start(out=ident[:], in_=ident_d[:, :]), writes=["ident"], dma=True)
        S.op("sp", lambda e: e.dma_start(out=tri[:], in_=tri_d[:, :]), writes=["tri"], dma=True)
        S.op("sp", lambda e: e.dma_start(out=bdtri[:], in_=bdtri_d[:, :]), writes=["bdtri"], dma=True)
        S.op("sp", lambda e: e.dma_start(out=flag[:], in_=flag_d[:, :]), writes=["flag"], dma=True)
        S.op("sp", lambda e: e.dma_start(out=vecs[:], in_=vecs_d[:, :, :]), writes=["vecs"], dma=True)
        S.op("sp", lambda e: e.dma_start(out=gfin[:], in_=gfin_d[:, :]), writes=["gfin"], dma=True)
        for l in range(L):
            S.op("act", lambda e, l=l: e.activation(out=esk[:, l, :], in_=vecs[:, l, 192:200], func=AF.Exp),
                 reads=["vecs"], writes=["esk"])
            for c in range(8):
                S.op("dve", lambda e, l=l, c=c: e.tensor_scalar(out=eskx[:, l, c * 8:(c + 1) * 8], in0=zer[:, :], scalar1=esk[:, l, c:c + 1],
                                                                scalar2=None, op0=ALU.add),
                     reads=["esk", "zer"], writes=["eskx"])
        for l in range(Lr):
            S.op("sp", lambda e, l=l: e.dma_start(out=kso_d[l, :, 0:120, :], in_=ck_d[l, :, 8:128, :]), writes=["kso_c%d" % l], dma=True)
            S.op("sp", lambda e, l=l: e.dma_start(out=vso_d[l, :, 0:120, :], in_=cv_d[l, :, 8:128, :]), writes=["vso_c%d" % l], dma=True)

        def col(l, i):
            return vecs[:, l, i:i + 1]

        def rmsnorm(gcol_fn, w0, T, tag):
            for c in range(8):
                S.op("act", lambda e, c=c: e.activation(out=gT[:, c, w0:w0 + T], in_=xT[:, c, w0:w0 + T], func=AF.Square),
                     reads=["xT%d" % c], writes=["gT%d" % c])
            for c in range(8):
                S.op("pe", lambda e, c=c: e.matmul(ps[6][:, 0:T], lhsT=ones[:, :], rhs=gT[:, c, w0:w0 + T], start=(c == 0), stop=(c == 7)),
                     reads=["ones", "gT%d" % c], writes=["ps6"])
            S.op("act", lambda e: e.activation(out=lnt[:, 0:T], in_=ps[6][:, 0:T], func=AF.Ln, scale=1.0 / D, bias=epsc[:, 0:1]),
                 reads=["ps6", "epsc"], writes=["lnt"])
            S.op("act", lambda e: e.activation(out=rstd[:, 0:T], in_=lnt[:, 0:T], func=AF.Exp, scale=-0.5),
                 reads=["lnt"], writes=["rstd"])
            return

        def apply_norm(dst, dname, gcol_fn, w0, T, out_w0):
            for c in range(8):
                S.op("dve", lambda e, c=c: e.scalar_tensor_tensor(out=dst[:, c, out_w0:out_w0 + T], in0=xT[:, c, w0:w0 + T], scalar=gcol_fn(c),
                                                                  in1=rstd[:, 0:T], op0=ALU.mult, op1=ALU.mult),
                     reads=["xT%d" % c, "rstd", "vecs", "gfin"], writes=[dname])

        def mmA(slot, sname, ncols, col0, kch, rhs, rname, w0, T, pst, pname):
            for c in range(kch):
                S.op("pe", lambda e, c=c: e.matmul(pst[:, 0:T], lhsT=slot[:, c * ncols + col0: c * ncols + col0 + 128],
                                                   rhs=rhs[:, c, w0:w0 + T], start=(c == 0), stop=(c == kch - 1)),
                     reads=[sname, rname], writes=[pname])

        evac_ctr = [0]

        def evac_copy(dst_ap, src_ap, reads, writes):
            evac_ctr[0] += 1
            if evac_ctr[0] % 2:
                S.op("act", lambda e: e.activation(out=dst_ap, in_=src_ap, func=AF.Identity), reads=reads, writes=writes)
            else:
                S.op("dve", lambda e: e.tensor_copy(out=dst_ap, in_=src_ap), reads=reads, writes=writes)

        def run_pass(kind, g, l, w0, T):
            nblk = T // 128
            _base = sidx[0]

            def stop(st_):
                if _stage_lim and st_ >= _stage_lim:
                    sidx[0] = _base + NT
                    return True
                return False
            last_own = (kind == "P" and g == NG - 1) and not _os.environ.get("KNOOUT")
            isS = (kind == "S")
            S.op("sp", lambda e: e.dma_start(out=lng[:], in_=lnp_d[l, 0]), writes=["lng"], dma=True)
            S.op("sp", lambda e: e.dma_start(out=lnb[:], in_=lnp_d[l, 1]), writes=["lnb"], dma=True)
            S.op("sp", lambda e: e.dma_start(out=wsf[:], in_=(wsS_d if isS else wsT_d)[l]), writes=["wsf"], dma=True)
            S.op("sp", lambda e: e.dma_start(out=bsb[:], in_=(bsS_d if isS else bsb_d)[l]), writes=["bsb"], dma=True)
            msk = bdtri if isS else tri
            for gg in range(6):
                S.op("pool", lambda e, gg=gg: e.tensor_tensor(out=wsb[:, gg, :], in0=wsf[:, gg, :], in1=msk[:, :], op=ALU.mult),
                     reads=["wsf", "tri", "bdtri"], writes=["wsb"])
            if isS:
                S.op("sp", lambda e: e.dma_start(out=sconv[:], in_=scT_d[l]), writes=["sconv"], dma=True)
            else:
                S.op("pool", lambda e: e.tensor_copy(out=kT[:, :, w0:w0 + 128], in_=kprev[:, l, :, :]), reads=["kprev"], writes=["kT"])
                S.op("pool", lambda e: e.tensor_copy(out=vb[:, w0 // 128, :], in_=vprev[:, l, :]), reads=["vprev"], writes=["vb"])

            rmsnorm(None, w0, T, "n1")
            apply_norm(hT, "hT", lambda c: col(l, c), w0, T, w0)

            if stop(1):
                return
            for t in range(2):
                slot, sname = next_tile()
                for j in range(4):
                    pst, pname = next_proj()
                    mmA(slot, sname, 512, j * 128, 8, hT, "hT", w0, T, pst, pname)
                    evac_copy(qT[:, 4 * t + j, w0:w0 + T], pst[:, 0:T], [pname], ["qT"])
            slot, sname = next_tile()
            for j in range(2):
                pst, pname = next_proj()
                mmA(slot, sname, 512, j * 128, 8, hT, "hT", w0, T, pst, pname)
                evac_copy(kT[:, j, 128 + w0:128 + w0 + T], pst[:, 0:T], [pname], ["kT"])
                if last_own:
                    S.op("act", lambda e, j=j, pst=pst: e.activation(out=kpo_t[:, j, :], in_=pst[:, T - 128:T], func=AF.Identity),
                         reads=[pname, "kT"], writes=["kpo_t"])
            if last_own:
                for j in range(2):
                    S.op("sp", lambda e, j=j: e.dma_start(out=kpo_d[l, j * 128:(j + 1) * 128, :], in_=kpo_t[:, j, :]), reads=["kpo_t"], writes=["kpo%d_%d" % (l, j)], dma=True)
            for bi in range(nblk):
                pst, pname = next_proj()
                cb = w0 + bi * 128
                for c in range(8):
                    S.op("pe", lambda e, c=c, pst=pst, cb=cb, slot=slot: e.matmul(pst[:, 0:256], lhsT=hT[:, c, cb:cb + 128], rhs=slot[:, c * 512 + 256:c * 512 + 512],
                                                                       start=(c == 0), stop=(c == 7)), reads=[sname, "hT"], writes=[pname])
                if isS:
                    evac_copy(vso_t[:, :], pst[:, 0:256], [pname], ["vso_t"])
                else:
                    evac_copy(vb[:, cb // 128 + 1, :], pst[:, 0:256], [pname], ["vb"])
                    if last_own and bi == nblk - 1:
                        S.op("act", lambda e, pst=pst: e.activation(out=vpo_t[:, :], in_=pst[:, 0:256], func=AF.Identity), reads=[pname, "vb"], writes=["vpo_t"])
                        S.op("sp", lambda e: e.dma_start(out=vpo_d[l], in_=vpo_t[:]), reads=["vpo_t"], writes=["vpo%d" % l], dma=True)
            if isS:
                pst, pname = next_proj()
                for c in range(8):
                    S.op("pe", lambda e, c=c, pst=pst, slot=slot: e.matmul(pst[:, 0:256], lhsT=hT[:, c, 0:128], rhs=slot[:, c * 512:c * 512 + 256],
                                                                start=(c == 0), stop=(c == 7)), reads=[sname, "hT"], writes=[pname])
                evac_copy(kso_t[:, :], pst[:, 0:256], [pname], ["kso_t"])
                for s in range(16):
                    S.op("sp", lambda e, s=s: e.dma_start(out=kso_d[l, s, 120:128, :], in_=kso_t[8 * s:8 * s + 8, :]), reads=["kso_t"], writes=["kso_n%d_%d" % (l, s)], dma=True)
                    S.op("sp", lambda e, s=s: e.dma_start(out=vso_d[l, s, 120:128, :], in_=vso_t[8 * s:8 * s + 8, :]), reads=["vso_t"], writes=["vso_n%d_%d" % (l, s)], dma=True)
                for s2 in range(8):
                    pst, pname = next_proj()
                    for ss in range(2):
                        s = 2 * s2 + ss
                        for c in range(8):
                            S.op("pe", lambda e, c=c, pst=pst, s=s, ss=ss, slot=slot: e.matmul(pst[0:8, ss * 256:ss * 256 + 256], lhsT=hT[:, c, 8 * s:8 * s + 8],
                                                                                    rhs=slot[:, c * 512 + 256:c * 512 + 512], start=(c == 0), stop=(c == 7)),
                                 reads=[sname, "hT"], writes=[pname])
                    evac_copy(vseq[0:8, 2 * s2:2 * s2 + 2, :], pst[0:8, 0:512].rearrange("p (s d) -> p s d", d=256), [pname], ["vseq"])
            if stop(2):
                return
            (t3, n3), (t4, n4), (t5, n5) = next_tiles(3)
            for j in range(6):
                pst, pname = next_proj()
                if j < 4:
                    mmA(t3, n3, 512, j * 128, 8, hT, "hT", w0, T, pst, pname)
                else:
                    mmA(t4, n4, 512, (j - 4) * 128, 8, hT, "hT", w0, T, pst, pname)
                S.op("act", lambda e, j=j, pst=pst: e.activation(out=guT[:, j, w0:w0 + T], in_=pst[:, 0:T], func=AF.Gelu_apprx_tanh),
                     reads=[pname], writes=["guT"])
            for bi in range(nblk):
                cb = w0 + bi * 128
                pa_, na_ = next_proj()
                for c in range(8):
                    S.op("pe", lambda e, c=c, pa_=pa_, cb=cb: e.matmul(pa_[:, 0:256], lhsT=hT[:, c, cb:cb + 128], rhs=t4[:, c * 512 + 256:c * 512 + 512],
                                                                       start=(c == 0), stop=(c == 7)), reads=[n4, "hT"], writes=[na_])
                S.op("act", lambda e, bi=bi, pa_=pa_: e.activation(out=gvf[:, bi, 0:256], in_=pa_[:, 0:256], func=AF.Gelu_apprx_tanh),
                     reads=[na_], writes=["gvf%d" % bi])
                pb_, nb_ = next_proj()
                for c in range(8):
                    S.op("pe", lambda e, c=c, pb_=pb_, cb=cb: e.matmul(pb_[:, 0:512], lhsT=hT[:, c, cb:cb + 128], rhs=t5[:, c * 512:c * 512 + 512],
                                                                       start=(c == 0), stop=(c == 7)), reads=[n5, "hT"], writes=[nb_])
                S.op("act", lambda e, bi=bi, pb_=pb_: e.activation(out=gvf[:, bi, 256:768], in_=pb_[:, 0:512], func=AF.Gelu_apprx_tanh),
                     reads=[nb_], writes=["gvf%d" % bi])
            def ln_steps(bi):
                def s0():
                    S.op("dve", lambda e: e.reduce_sum(out=lnst[:, bi:bi + 1], in_=gvf[:, bi, :], axis=mybir.AxisListType.X),
                         reads=["gvf%d" % bi], writes=["lnst_a%d" % bi])

                def s1():
                    S.op("dve", lambda e: e.tensor_scalar(out=lnst[:, 4 + bi:5 + bi], in0=lnst[:, bi:bi + 1], scalar1=-1.0 / 768, scalar2=None, op0=ALU.mult),
                         reads=["lnst_a%d" % bi], writes=["lnst_b%d" % bi])
                    S.op("dve", lambda e: e.tensor_scalar(out=gvf[:, bi, :], in0=gvf[:, bi, :], scalar1=lnst[:, 4 + bi:5 + bi], scalar2=None, op0=ALU.add),
                         reads=["lnst_b%d" % bi, "gvf%d" % bi], writes=["gvf%d" % bi])

                def s2():
                    S.op("pool", lambda e: e.tensor_tensor(out=xc[:, :], in0=gvf[:, bi, :], in1=gvf[:, bi, :], op=ALU.mult),
                         reads=["gvf%d" % bi], writes=["xc"])

                def s3():
                    S.op("dve", lambda e: e.reduce_sum(out=lnst[:, 8 + bi:9 + bi], in_=xc[:, :], axis=mybir.AxisListType.X),
                         reads=["xc"], writes=["lnst_c%d" % bi])

                def s4():
                    S.op("act", lambda e: e.activation(out=lnst[:, 12 + bi:13 + bi], in_=lnst[:, 8 + bi:9 + bi], func=AF.Ln, scale=1.0 / 768, bias=epsc[:, 0:1]),
                         reads=["lnst_c%d" % bi, "epsc"], writes=["lnst_d%d" % bi])
                    S.op("act", lambda e: e.activation(out=lnst[:, 12 + bi:13 + bi], in_=lnst[:, 12 + bi:13 + bi], func=AF.Exp, scale=-0.5),
                         reads=["lnst_d%d" % bi], writes=["lnst_d%d" % bi])

                def s5():
                    S.op("dve", lambda e: e.scalar_tensor_tensor(out=gvf[:, bi, :], in0=gvf[:, bi, :], scalar=lnst[:, 12 + bi:13 + bi], in1=lng[:, :],
                                                                 op0=ALU.mult, op1=ALU.mult), reads=["lnst_d%d" % bi, "lng", "gvf%d" % bi], writes=["gvf%d" % bi])

                def s6():
                    if isS:
                        S.op("pool", lambda e: e.tensor_tensor(out=gvf[:, bi, :], in0=gvf[:, bi, :], in1=lnb[:, :], op=ALU.add),
                             reads=["lnb", "gvf%d" % bi], writes=["gvf%d" % bi])
                        S.op("sp", lambda e: e.dma_start(out=gvo_d[l], in_=gvf[:, 0, :]), reads=["gvf0", "c0_0", "c0_1"], writes=["gvo%d" % l], dma=True)
                        S.op("pool", lambda e: e.tensor_copy(out=vnb[:, bi, :], in_=gvf[:, bi, :]), reads=["gvf%d" % bi], writes=["vnb%d" % bi])
                    else:
                        S.op("pool", lambda e: e.tensor_tensor(out=vnb[:, bi, :], in0=gvf[:, bi, :], in1=lnb[:, :], op=ALU.add),
                             reads=["lnb", "gvf%d" % bi], writes=["vnb%d" % bi])
                return [s0, s1, s2, s3, s4, s5, s6]

            if stop(3):
                return
            def attn_unit_prompt(bi, hooks=()):
                blk = w0 // 128 + bi
                qc0 = w0 + bi * 128
                mk, mkn = (m2f, "m2f") if (g == G0 and blk == 0) else (m2, "m2")

                def ctx(c):
                    par = c % 4
                    ob = (par % 2) * 256
                    obank, obname = (ps[5], "psO%d" % par) if par < 2 else (ps[2], "ps2")
                    return par, ob, obank, obname

                def stA(c):
                    par, ob, obank, obname = ctx(c)
                    for h in range(2):
                        bank = (ps[3 + h] if par < 2 else ps[6 + h])
                        bname = "psS%d_%d" % (h, par) if par < 2 else ("ps6" if h == 0 else "ps7")
                        cbs = (par % 2) * 256
                        hp = slice(64 * h, 64 * h + 64)
                        S.op("pe", lambda e, bank=bank, cbs=cbs: e.matmul(bank[:, cbs:cbs + 256], lhsT=ident[:, :], rhs=mk[:, :], start=True, stop=False),
                             reads=["ident", mkn], writes=[bname])
                        S.op("pe", lambda e, bank=bank, hp=hp, c=c, cbs=cbs: e.matmul(bank[:, cbs:cbs + 128], lhsT=kT[hp, c // 4, qc0:qc0 + 128],
                                                                                      rhs=qT[hp, c, qc0:qc0 + 128], start=False, stop=False),
                             reads=["kT", "qT"], writes=[bname])
                        S.op("pe", lambda e, bank=bank, hp=hp, c=c, cbs=cbs: e.matmul(bank[:, cbs + 128:cbs + 256], lhsT=kT[hp, c // 4, qc0 + 128:qc0 + 256],
                                                                                      rhs=qT[hp, c, qc0:qc0 + 128], start=False, stop=True),
                             reads=["kT", "qT"], writes=[bname])
                        pt, ptn = pT[h][par], "pT%d%d" % (h, par)
                        S.op("act", lambda e, bank=bank, pt=pt, cbs=cbs: e.activation(out=pt[:, :], in_=bank[:, cbs:cbs + 256], func=AF.Exp, scale=0.125),
                             reads=[bname], writes=[ptn])

                def stB(c):
                    par, ob, obank, obname = ctx(c)
                    for h in range(2):
                        pt, ptn = pT[h][par], "pT%d%d" % (h, par)
                        gk = 2 * (c // 4) + h
                        op_ = obank[64 * h:64 * h + 64, ob:ob + 128]
                        dn_ = obank[64 * h:64 * h + 64, ob + 128:ob + 256]
                        S.op("pe", lambda e, op_=op_, pt=pt, gk=gk: e.matmul(op_, lhsT=vb[:, blk, gk * 64:gk * 64 + 64], rhs=pt[:, 0:128], start=True, stop=False),
                             reads=["vb", ptn], writes=[obname])
                        S.op("pe", lambda e, op_=op_, pt=pt, gk=gk: e.matmul(op_, lhsT=vb[:, blk + 1, gk * 64:gk * 64 + 64], rhs=pt[:, 128:256], start=False, stop=True),
                             reads=["vb", ptn], writes=[obname])
                        S.op("pe", lambda e, dn_=dn_, pt=pt: e.matmul(dn_, lhsT=ones[:, 0:64], rhs=pt[:, 0:128], start=True, stop=False),
                             reads=["ones", ptn], writes=[obname])
                        S.op("pe", lambda e, dn_=dn_, pt=pt: e.matmul(dn_, lhsT=ones[:, 0:64], rhs=pt[:, 128:256], start=False, stop=True),
                             reads=["ones", ptn], writes=[obname])
                    rd, rdn = rden[par], "rden%d" % par
                    S.op("dve", lambda e, rd=rd, c=c, ob=ob, obank=obank: e.tensor_scalar(out=rd[:, :], in0=obank[:, ob + 128:ob + 256], scalar1=esk[:, l, c:c + 1], scalar2=None, op0=ALU.add),
                         reads=[obname, "esk"], writes=[rdn])

                def stC(c):
                    par, ob, obank, obname = ctx(c)
                    rd, rdn = rden[par], "rden%d" % par
                    S.op("act", lambda e, rd=rd: e.activation(out=rd[:, :], in_=rd[:, :], func=AF.Ln), reads=[rdn], writes=[rdn])
                    S.op("act", lambda e, rd=rd: e.activation(out=rd[:, :], in_=rd[:, :], func=AF.Exp, scale=-1.0), reads=[rdn], writes=[rdn])

                def stD(c):
                    par, ob, obank, obname = ctx(c)
                    rd, rdn = rden[par], "rden%d" % par
                    S.op("dve", lambda e, rd=rd, c=c, ob=ob, obank=obank: e.tensor_tensor(out=oT[:, c, qc0:qc0 + 128], in0=obank[:, ob:ob + 128], in1=rd[:, :], op=ALU.mult),
                         reads=[obname, rdn], writes=["oT"])

                dB, dC, dD = [int(x) for x in _os.environ.get("KOFF", "1,2,4").split(",")]
                for t in range(8 + dD):
                    if t < 8:
                        stA(t)
                    if 0 <= t - dB < 8:
                        stB(t - dB)
                    if 0 <= t - dC < 8:
                        stC(t - dC)
                    if 0 <= t - dD < 8:
                        stD(t - dD)
                    if t < len(hooks):
                        hooks[t]()

            def attn_unit_sample(s):
                par = s % 2
                ob = par * 256
                q0 = 8 * s
                for h in range(2):
                    bank, bname = ps[3 + h], "psS%d_%d" % (h, par)
                    cbs = par * 256
                    hp = slice(64 * h, 64 * h + 64)
                    S.op("pe", lambda e, bank=bank: e.matmul(bank[:, cbs:cbs + 128], lhsT=ident[:, :], rhs=sms[:, :], start=True, stop=False),
                         reads=["ident", "sms"], writes=[bname])
                    for k2 in range(2):
                        qv = qT[hp, 4 * k2:4 * k2 + 4, q0:q0 + 8]
                        S.op("pe", lambda e, bank=bank, hp=hp, k2=k2, qv=qv: e.matmul(bank[:, cbs + k2 * 32:cbs + k2 * 32 + 32], lhsT=ckTs[hp, s % 8, k2, :],
                                                                                      rhs=qv, start=False, stop=False),
                             reads=["ckTs", "qT"], writes=[bname])
                        S.op("pe", lambda e, bank=bank, hp=hp, k2=k2, qv=qv: e.matmul(bank[0:8, cbs + 64 + k2 * 32:cbs + 64 + k2 * 32 + 32], lhsT=kT[hp, k2, 128 + q0:128 + q0 + 8],
                                                                                      rhs=qv, start=False, stop=(k2 == 1)),
                             reads=["kT", "qT"], writes=[bname])
                    pt, ptn = pT[h][par], "pT%d%d" % (h, par)
                    S.op("act", lambda e, bank=bank, pt=pt: e.activation(out=pt[:, 0:128], in_=bank[:, cbs:cbs + 128], func=AF.Exp, scale=0.125),
                         reads=[bname], writes=[ptn])
                pname = "psO%d" % par
                for h in range(2):
                    pt, ptn = pT[h][par], "pT%d%d" % (h, par)
                    for k2 in range(2):
                        gk = 2 * k2 + h
                        op_ = ps[5][64 * h:64 * h + 64, ob + k2 * 32:ob + k2 * 32 + 32]
                        dn_ = ps[5][64 * h:64 * h + 64, ob + 64 + k2 * 32:ob + 64 + k2 * 32 + 32]
                        S.op("pe", lambda e, op_=op_, pt=pt, gk=gk, k2=k2: e.matmul(op_, lhsT=cvs[:, s % 8, gk * 64:gk * 64 + 64], rhs=pt[:, k2 * 32:k2 * 32 + 32], start=True, stop=False),
                             reads=["cvs", ptn], writes=[pname])
                        S.op("pe", lambda e, op_=op_, pt=pt, gk=gk, k2=k2: e.matmul(op_, lhsT=vseq[0:8, s, gk * 64:gk * 64 + 64], rhs=pt[0:8, 64 + k2 * 32:64 + k2 * 32 + 32], start=False, stop=True),
                             reads=["vseq", ptn], writes=[pname])
                        S.op("pe", lambda e, dn_=dn_, pt=pt, k2=k2: e.matmul(dn_, lhsT=ones[:, 0:64], rhs=pt[:, k2 * 32:k2 * 32 + 32], start=True, stop=False),
                             reads=["ones", ptn], writes=[pname])
                        S.op("pe", lambda e, dn_=dn_, pt=pt, k2=k2: e.matmul(dn_, lhsT=ones[0:8, 0:64], rhs=pt[0:8, 64 + k2 * 32:64 + k2 * 32 + 32], start=False, stop=True),
                             reads=["ones", ptn], writes=[pname])
                rd, rdn = rden[par], "rden%d" % par
                S.op("dve", lambda e, rd=rd: e.tensor_tensor(out=rd[:, 0:64], in0=ps[5][:, ob + 64:ob + 128], in1=eskx[:, l, :], op=ALU.add),
                     reads=[pname, "eskx"], writes=[rdn])
                S.op("dve", lambda e, rd=rd: e.reciprocal(out=rd[:, 0:64], in_=rd[:, 0:64]), reads=[rdn], writes=[rdn])
                S.op("dve", lambda e, rd=rd: e.tensor_tensor(out=oT[:, :, q0:q0 + 8], in0=ps[5][:, ob:ob + 64].rearrange("p (c q) -> p c q", q=8),
                                                             in1=rd[:, 0:64].rearrange("p (c q) -> p c q", q=8), op=ALU.mult),
                     reads=[pname, rdn], writes=["oT"])

            if isS:
                for s in range(16):
                    if 2 <= s < 9:
                        ln_steps(0)[s - 2]()
                    if s % 8 == 0:
                        S.op("pool", lambda e, s=s: e.dma_start(out=ckTs[:], in_=ckT_d[l, :, s:s + 8]), writes=["ckTs"], dma=True)
                        S.op("pool", lambda e, s=s: e.dma_start(out=cvs[:], in_=cv_d[l, s:s + 8].rearrange("s k d -> k s d")), writes=["cvs"], dma=True)
                    attn_unit_sample(s)
            else:
                for bi in range(nblk):
                    hk = ln_steps(bi)
                    if _os.environ.get("KHOOK", "0") == "1":
                        attn_unit_prompt(bi, [(lambda: None), hk[0], hk[1], hk[2], hk[3], hk[4], hk[5], hk[6]])
                    else:
                        attn_unit_prompt(bi, [])
                        for f_ in hk:
                            f_()

            if stop(4):
                return
            for bi in range(nblk):
                cb = w0 + bi * 128
                for gg in range(6):
                    par = gg % 2
                    pbank, pname = next_proj()
                    pst = pbank[:, 0:128]
                    S.op("pe", lambda e, pst=pst, bi=bi, gg=gg: e.matmul(pst, lhsT=vnb[:, bi, gg * 128:gg * 128 + 128], rhs=wsb[:, gg, :], start=True, stop=True),
                         reads=["vnb%d" % bi, "wsb"], writes=[pname])
                    mt, mtn = mixt[par], "mixt%d" % par
                    S.op("dve", lambda e, pst=pst, mt=mt, gg=gg: e.tensor_tensor(out=mt[:, :], in0=pst, in1=bsb[:, gg, :], op=ALU.add),
                         reads=[pname, "bsb"], writes=[mtn])
                    S.op("pool", lambda e, mt=mt, gg=gg, cb=cb: e.tensor_tensor(out=sT[:, gg, cb:cb + 128], in0=mt[:, :], in1=guT[:, gg, cb:cb + 128], op=ALU.mult),
                         reads=[mtn, "guT"], writes=["guT"])

            if stop(5):
                return
            for hh in range(2):
                (tga, nga), (tgb, ngb), (tpa, npa), (tpb, npb) = next_tiles(4)
                GV = ["gvf0", "gvf1", "gvf2", "gvf3"]
                for j in range(4):
                    jj = 4 * hh + j
                    alt = (jj % 2 == 1)
                    tgA, tgAn, xa = (c0[0], "c0_0", GV) if alt else (tg[0], "tg0", [])
                    tgB, tgBn, xb = (c0[1], "c0_1", GV) if alt else (tg[1], "tg1", [])
                    mA, mAn, xc_ = (c0[2], "c0_2", GV) if alt else (m1, "m1", [])
                    mB, mBn, xd = (c0[3], "c0_3", GV) if alt else (m2t, "m2t", [])
                    p1, n1 = next_proj()
                    mmA(tga, nga, 512, j * 128, 8, hT, "hT", w0, T, p1, n1)
                    S.op("act", lambda e, p1=p1, tgA=tgA: e.activation(out=tgA[:, 0:T], in_=p1[:, 0:T], func=AF.Tanh, scale=0.5), reads=[n1], writes=[tgAn] + xa)
                    p2, n2 = next_proj()
                    mmA(tgb, ngb, 512, j * 128, 8, hT, "hT", w0, T, p2, n2)
                    S.op("act", lambda e, p2=p2, tgB=tgB: e.activation(out=tgB[:, 0:T], in_=p2[:, 0:T], func=AF.Tanh, scale=0.5), reads=[n2], writes=[tgBn] + xb)
                    p3, n3_ = next_proj()
                    mmA(tpa, npa, 512, j * 128, 8, oT, "oT", w0, T, p3, n3_)
                    S.op("dve", lambda e, p3=p3, tgA=tgA, mA=mA: e.scalar_tensor_tensor(out=mA[:, 0:T], in0=tgA[:, 0:T], scalar=1.0, in1=p3[:, 0:T], op0=ALU.add, op1=ALU.mult),
                         reads=[tgAn, n3_], writes=[mAn] + xc_)
                    p4, n4_ = next_proj()
                    mmA(tpb, npb, 512, j * 128, 6, sT, "guT", w0, T, p4, n4_)
                    S.op("dve", lambda e, p4=p4, tgB=tgB, mB=mB: e.scalar_tensor_tensor(out=mB[:, 0:T], in0=tgB[:, 0:T], scalar=1.0, in1=p4[:, 0:T], op0=ALU.add, op1=ALU.mult),
                         reads=[tgBn, n4_], writes=[mBn] + xd)
                    S.op("pool", lambda e, jj=jj, mA=mA, mB=mB: e.tensor_tensor(out=mT[:, jj, w0:w0 + T], in0=mA[:, 0:T], in1=mB[:, 0:T], op=ALU.add),
                         reads=[mAn, mBn], writes=["qT"])
            if stop(6):
                return
            for t in range(2):
                slot, sname = next_tile()
                for j in range(4):
                    jj = 4 * t + j
                    pst, pname = next_proj()
                    mmA(slot, sname, 512, j * 128, 8, mT, "qT", w0, T, pst, pname)
                    S.op("dve", lambda e, pst=pst, jj=jj: e.scalar_tensor_tensor(out=xT[:, jj, w0:w0 + T], in0=pst[:, 0:T], scalar=0.5, in1=xT[:, jj, w0:w0 + T],
                                                                                 op0=ALU.mult, op1=ALU.add), reads=[pname, "xT%d" % jj], writes=["xT%d" % jj])
            if stop(7):
                return
            rmsnorm(None, w0, T, "n2")
            apply_norm(hT, "hT", lambda c: col(l, 8 + c), w0, T, w0)

            if stop(8):
                return
            nseg = 16 if isS else 1
            sl = T // nseg
            deferred = []
            for i in range(11):
                slot, sname = next_tile()
                for q in range(4):
                    pc = 4 * i + q
                    r = pc % 6
                    pst, pname = next_proj()
                    mmA(slot, sname, 512, q * 128, 8, hT, "hT", w0, T, pst, pname)
                    z, zn = zb[pc % 2], "zb%d" % (pc % 2)
                    zv = z[:, 0:nseg * (sl + 2)].rearrange("p (s t) -> p s t", t=sl + 2)
                    pv = pst[:, 0:T].rearrange("p (s t) -> p s t", t=sl)
                    if isS:
                        S.op("pool", lambda e, zv=zv, pc=pc: e.tensor_copy(out=zv[:, :, 0:2], in_=sconv[:, pc, :].rearrange("p (s t) -> p s t", t=2)),
                             reads=["sconv"], writes=[zn + "h"])
                    else:
                        S.op("pool", lambda e, zv=zv, pc=pc: e.tensor_copy(out=zv[:, :, 0:2], in_=ztail[:, l, pc, :].rearrange("p (s t) -> p s t", t=2)),
                             reads=["ztail%d" % l], writes=[zn + "h"])
                    S.op("act", lambda e, zv=zv, pv=pv: e.activation(out=zv[:, :, 2:2 + sl], in_=pv, func=AF.Identity), reads=[pname], writes=[zn])
                    cc, cn = c0[r], "c0_%d" % r
                    S.op("act", lambda e, cc=cc, pst=pst, pc=pc: e.activation(out=cc[:, 0:T], in_=pst[:, 0:T], func=AF.Identity,
                                                                              scale=col(l, 16 + 88 + pc), bias=col(l, 16 + 132 + pc)),
                         reads=[pname, "vecs"], writes=[cn])
                    ccv = cc[:, 0:T].rearrange("p (s t) -> p s t", t=sl)
                    S.op("dve", lambda e, ccv=ccv, zv=zv, pc=pc: e.scalar_tensor_tensor(out=ccv, in0=zv[:, :, 1:1 + sl], scalar=col(l, 16 + 44 + pc), in1=ccv,
                                                                                        op0=ALU.mult, op1=ALU.add), reads=[zn, zn + "h", cn, "vecs"], writes=[cn])
                    if isS:
                        S.op("pool", lambda e, zv=zv, pc=pc: e.tensor_copy(out=zso_t[:, pc, :].rearrange("p (s t) -> p s t", t=2), in_=zv[:, :, sl:sl + 2]),
                             reads=[zn], writes=["sconv"])
                    else:
                        S.op("pool", lambda e, z=z, pc=pc: e.tensor_copy(out=ztail[:, l, pc, :], in_=z[:, T:T + 2]),
                             reads=[zn, zn + "h"], writes=["ztail%d" % l])
                    S.op("dve", lambda e, ccv=ccv, zv=zv, pc=pc: e.scalar_tensor_tensor(out=ccv, in0=zv[:, :, 0:sl], scalar=col(l, 16 + pc), in1=ccv,
                                                                                         op0=ALU.mult, op1=ALU.add), reads=[zn, zn + "h", cn, "vecs"], writes=[cn])
                    if q < 2:
                        deferred.append((pc + 3, (lambda cc=cc, cn=cn: S.op("act", lambda e: e.activation(out=cc[:, 0:T], in_=cc[:, 0:T], func=AF.Gelu_apprx_tanh),
                                                                              reads=[cn], writes=[cn]))))
                    else:
                        ja = 2 * i + q - 2
                        ra, rb = (pc - 2) % 6, pc % 6
                        deferred.append((pc + 2, (lambda ra=ra, rb=rb, ja=ja: S.op(
                            "pool", lambda e: e.tensor_tensor(out=gT[:, ja, w0:w0 + T], in0=c0[ra][:, 0:T], in1=c0[rb][:, 0:T], op=ALU.mult),
                            reads=["c0_%d" % ra, "c0_%d" % rb], writes=["gT%d" % ja]))))
                    deferred.sort(key=lambda d_: d_[0])
                    while deferred and deferred[0][0] <= pc:
                        deferred.pop(0)[1]()
            while deferred:
                deferred.pop(0)[1]()
            if isS:
                S.op("sp", lambda e: e.dma_start(out=zso_d[l], in_=zso_t[:]), reads=["sconv"], writes=["zso%d" % l], dma=True)
            if stop(9):
                return
            for j in range(8):
                slot, sname = next_tile()
                pst, pname = next_proj()
                for c in range(22):
                    S.op("pe", lambda e, c=c, pst=pst, slot=slot: e.matmul(pst[:, 0:T], lhsT=slot[:, c * 128:c * 128 + 128], rhs=gT[:, c, w0:w0 + T],
                                                                           start=(c == 0), stop=(c == 21)), reads=[sname, "gT%d" % c], writes=[pname])
                S.op("dve", lambda e, pst=pst, j=j: e.tensor_tensor(out=xT[:, j, w0:w0 + T], in0=pst[:, 0:T], in1=xT[:, j, w0:w0 + T], op=ALU.add),
                     reads=[pname, "xT%d" % j], writes=["xT%d" % j])
            if not isS:
                S.op("pool", lambda e: e.tensor_copy(out=kprev[:, l, :, :], in_=kT[:, :, w0 + T:w0 + T + 128]), reads=["kT"], writes=["kprev"])
                S.op("pool", lambda e: e.tensor_copy(out=vprev[:, l, :], in_=vb[:, (w0 + T) // 128, :]), reads=["vb"], writes=["vprev"])
                if g == G0 - 1:
                    S.op("pool", lambda e: e.tensor_scalar(out=ztail[:, l, :, :], in0=ztail[:, l, :, :], scalar1=flag[:, 0:1], scalar2=None, op0=ALU.mult),
                         reads=["ztail%d" % l, "flag"], writes=["ztail%d" % l])
                if last_own:
                    S.op("sp", lambda e: e.dma_start(out=zpo_d[l].rearrange("p c t -> p (c t)"), in_=ztail[:, l, :, :].rearrange("p c t -> p (c t)")), reads=["ztail%d" % l], writes=["zpo%d" % l], dma=True)

        def finalize(kind, g):
            if _os.environ.get("KNOFIN"):
                return
            T = 128 if kind == "S" else 512
            rmsnorm(None, 0, T, "nf")
            o0 = (g - G0) * 512
            for c in range(8):
                yb, ybn = c0[c % 4], "c0_%d" % (c % 4)
                S.op("dve", lambda e, c=c, yb=yb: e.scalar_tensor_tensor(out=yb[:, 0:T], in0=xT[:, c, 0:T], scalar=gfin[:, c:c + 1], in1=rstd[:, 0:T],
                                                                         op0=ALU.mult, op1=ALU.mult), reads=["xT%d" % c, "rstd", "gfin"], writes=[ybn])
                if kind == "S":
                    S.op("sp", lambda e, c=c, yb=yb: e.dma_start(out=ysT_d[c * 128:(c + 1) * 128, :], in_=yb[:, 0:128]), reads=[ybn, "gvf0", "gvf1", "gvf2", "gvf3"], writes=["ysT%d" % c], dma=True)
                else:
                    S.op("sp", lambda e, c=c, yb=yb: e.dma_start(out=yT_d[c * 128:(c + 1) * 128, o0:o0 + 512], in_=yb[:, 0:512]), reads=[ybn, "gvf0", "gvf1", "gvf2", "gvf3"], writes=["yT%d_%d" % (g, c)], dma=True)

        cur = None
        for (kind, g, l, w0, T) in passes:
            if (kind, g) != cur:
                if cur is not None and (cur[0] == "S" or cur[1] >= G0):
                    finalize(*cur)
                cur = (kind, g)
                if kind == "S":
                    S.op("sp", lambda e: e.dma_start(out=xT[:, :, 0:128], in_=xsT_d.rearrange("(c p) t -> p c t", p=128)),
                         writes=["xT%d" % c for c in range(8)], dma=True)
                else:
                    S.op("sp", lambda e, g=g: e.dma_start(out=xT[:, :, :], in_=xT_d[:, g * 512:(g + 1) * 512].rearrange("(c p) t -> p c t", p=128)),
                         writes=["xT%d" % c for c in range(8)], dma=True)
            run_pass(kind, g, l, w0, T)
        if cur is not None and (cur[0] == "S" or cur[1] >= G0):
            finalize(*cur)
        S.emit()
    return nc


def _tileA(W, col0, ncols, kch):
    a = W[:, col0:col0 + ncols].reshape(kch, 128, ncols).transpose(1, 0, 2).reshape(128, kch * ncols)
    out = np.zeros((128, TE), np.float32)
    out[:, :kch * ncols] = a
    return out


def _prep_weights(w_in, w_pa, w_pb, w_out, w_up, w_down):
    L = w_in.shape[0]
    qperm = []
    for c in range(8):
        for hd in (QA[c], QB[c]):
            qperm.extend(range(hd * 64, hd * 64 + 64))
    qperm = np.array(qperm)
    upperm = []
    for i in range(11):
        for q in range(4):
            base = (2 * i + q) * 128 if q < 2 else 2816 + (2 * i + q - 2) * 128
            upperm.extend(range(base, base + 128))
    upperm = np.array(upperm)
    wt = np.zeros((L, NT, 128, TE), np.float32)
    for l in range(L):
        Wi = w_in[l]
        Wp = np.concatenate([Wi[:, 0:1024][:, qperm], Wi[:, 1024:]], axis=1)
        tl = []
        for t in range(6):
            tl.append(_tileA(Wp, t * 512, 512, 8))
        pa = w_pa[l][qperm, :]
        for hh in range(2):
            tl.append(_tileA(Wp, 3072 + hh * 512, 512, 8))
            tl.append(_tileA(Wp, 4096 + hh * 512, 512, 8))
            tl.append(_tileA(pa, hh * 512, 512, 8))
            tl.append(_tileA(w_pb[l], hh * 512, 512, 6))
        for t in range(2):
            tl.append(_tileA(w_out[l], t * 512, 512, 8))
        Wu = w_up[l][:, upperm]
        for i in range(11):
            tl.append(_tileA(Wu, i * 512, 512, 8))
        for j in range(8):
            tl.append(_tileA(w_down[l], j * 128, 128, 22))
        assert len(tl) == NT
        wt[l] = np.stack(tl)
    return wt, qperm, upperm


_PROG_CACHE = {}


def kernel(x_prompt, x_sample, cache_win_k, cache_win_v, state_conv, norm_mix, w_in, sinks, gmlp_ln_g, gmlp_ln_b,
           gmlp_ws, gmlp_bs, w_branch_attn, w_branch_gmlp, w_out, norm_ffn, w_up, conv_w, conv_b, w_down, norm_final,
           _nlayers_run=None):
    f = lambda a: np.ascontiguousarray(np.asarray(a, dtype=np.float32))
    x_prompt, x_sample, cache_win_k, cache_win_v, state_conv = map(f, (x_prompt, x_sample, cache_win_k, cache_win_v, state_conv))
    norm_mix, w_in, sinks, gmlp_ln_g, gmlp_ln_b, gmlp_ws, gmlp_bs = map(f, (norm_mix, w_in, sinks, gmlp_ln_g, gmlp_ln_b, gmlp_ws, gmlp_bs))
    w_branch_attn, w_branch_gmlp, w_out, norm_ffn, w_up, conv_w, conv_b, w_down, norm_final = map(
        f, (w_branch_attn, w_branch_gmlp, w_out, norm_ffn, w_up, conv_w, conv_b, w_down, norm_final))
    B, SEQ, _ = x_prompt.shape
    L = w_in.shape[0]
    NSEQ = x_sample.shape[0]
    cpb = NCORES // B
    nown = SEQ // cpb // 128
    halo = ((2 * L + 3) // 4) * 4
    NB = nown + halo
    TP = NB * 128
    spc = NSEQ // NCORES
    assert spc == 16 and x_sample.shape[1] == 8

    wt, qperm, upperm = _prep_weights(w_in, w_branch_attn, w_branch_gmlp, w_out, w_up, w_down)

    def colmajor(v, n):
        return v.reshape(n, 128).T

    vecs = np.zeros((128, L, NV), np.float32)
    for l in range(L):
        vecs[:, l, 0:8] = colmajor(norm_mix[l], 8)
        vecs[:, l, 8:16] = colmajor(norm_ffn[l], 8)
        for j in range(3):
            vecs[:, l, 16 + 44 * j:16 + 44 * (j + 1)] = colmajor(conv_w[l, j][upperm], 44)
        vecs[:, l, 148:192] = colmajor(conv_b[l][upperm], 44)
        for c in range(8):
            vecs[0:64, l, 192 + c] = sinks[l, QA[c]]
            vecs[64:128, l, 192 + c] = sinks[l, QB[c]]
    gfin = np.ascontiguousarray(colmajor(norm_final, 8))
    lnp = np.ascontiguousarray(np.broadcast_to(np.stack([gmlp_ln_g, gmlp_ln_b], axis=1)[:, :, None, :], (L, 2, 128, 768)))
    wsT = np.ascontiguousarray(gmlp_ws.transpose(0, 3, 1, 2))
    ws8 = gmlp_ws[:, :, :8, :8]
    wsS = np.ascontiguousarray(np.tile(ws8.transpose(0, 3, 1, 2), (1, 16, 1, 16)))
    bsb = np.ascontiguousarray(np.broadcast_to(gmlp_bs[:, None, :, :], (L, 128, 6, 128)))
    bsS = np.ascontiguousarray(np.broadcast_to(np.tile(gmlp_bs[:, :, :8], (1, 1, 16))[:, None, :, :], (L, 128, 6, 128)))
    kk = np.arange(128)[:, None]
    qq = np.arange(128)[None, :]
    NEGM = -30000.0
    m2 = np.where(np.concatenate([(kk > qq), (kk <= qq)], axis=1), 0.0, NEGM).astype(np.float32)
    m2z = m2.copy()
    m2z[:, :128] = NEGM
    tri = (kk <= qq).astype(np.float32)
    bdtri = ((kk // 8 == qq // 8) & (kk <= qq)).astype(np.float32)
    sms = np.zeros((128, 2, 8, 8), np.float32)
    k8 = np.arange(128)[:, None]
    q8 = np.arange(8)[None, :]
    sms[:, 0, :, :] = np.where(k8 > q8, 0.0, NEGM).astype(np.float32)[:, None, :]
    sms[:, 1, :, :] = np.where((k8 <= q8) & (k8 < 8), 0.0, NEGM).astype(np.float32)[:, None, :]
    sms = sms.reshape(128, 128)

    in_maps = []
    for core in range(NCORES):
        b = core // cpb
        r = core % cpb
        t0 = r * nown * 128 - halo * 128
        xs = np.zeros((TP, D), np.float32)
        lo = max(t0, 0)
        xs[lo - t0:] = x_prompt[b, lo:t0 + TP]
        s0 = core * spc
        ck = cache_win_k[:, s0:s0 + spc].reshape(L, spc, 128, 256)
        cv = cache_win_v[:, s0:s0 + spc].reshape(L, spc, 128, 256)
        ckT = np.ascontiguousarray(ck.reshape(L, spc, 128, 2, 128).transpose(0, 4, 1, 3, 2))
        sc = state_conv[:, s0:s0 + spc][:, :, :, upperm]
        scT = np.ascontiguousarray(sc.reshape(L, spc, 2, 44, 128).transpose(0, 4, 3, 1, 2).reshape(L, 128, 44, 32))
        first = (r == 0)
        in_maps.append({
            "xT": np.ascontiguousarray(xs.T),
            "xsT": np.ascontiguousarray(x_sample[s0:s0 + spc].reshape(spc * 8, D).T),
            "wt": wt, "vecs": vecs, "gfin": gfin, "lnp": lnp, "wsT": wsT, "wsS": wsS, "bsb": bsb, "bsS": bsS,
            "m2": m2, "m2f": (m2z if first else m2), "tri": tri, "bdtri": bdtri, "sms": sms, "ident": np.eye(128, dtype=np.float32),
            "flag": np.full((128, 1), 0.0 if first else 1.0, np.float32),
            "ck": np.ascontiguousarray(ck), "cv": np.ascontiguousarray(cv), "ckT": ckT, "scT": scT,
        })

    key = (nown, L, _nlayers_run)
    if key not in _PROG_CACHE:
        _PROG_CACHE[key] = build_program(nown, L, _nlayers_run)
    nc = _PROG_CACHE[key]
    res = run_bass_kernel_spmd(nc, in_maps, core_ids=list(range(NCORES)))
    R = res.results

    inv_up = np.argsort(upperm)
    y_prompt = np.zeros((B, SEQ, D), np.float32)
    y_sample = np.zeros((NSEQ, 8, D), np.float32)
    wkp = np.zeros((L, B, 128, 4, 64), np.float32)
    wvp = np.zeros((L, B, 128, 4, 64), np.float32)
    cvp = np.zeros((L, B, 2, 5632), np.float32)
    wks = np.zeros((L, NSEQ, 128, 4, 64), np.float32)
    wvs = np.zeros((L, NSEQ, 128, 4, 64), np.float32)
    cvs_ = np.zeros((L, NSEQ, 2, 5632), np.float32)
    gvs = np.zeros((L, NSEQ, 8, 768), np.float32)
    for core in range(NCORES):
        b = core // cpb
        r = core % cpb
        o = R[core]
        y_prompt[b, r * nown * 128:(r + 1) * nown * 128] = o["yT"].T
        s0 = core * spc
        y_sample[s0:s0 + spc] = o["ysT"].T.reshape(spc, 8, D)
        wks[:, s0:s0 + spc] = o["kso"].reshape(L, spc, 128, 4, 64)
        wvs[:, s0:s0 + spc] = o["vso"].reshape(L, spc, 128, 4, 64)
        zs = o["zso"].reshape(L, 128, 44, spc, 2).transpose(0, 3, 4, 2, 1).reshape(L, spc, 2, 5632)
        cvs_[:, s0:s0 + spc] = zs[:, :, :, inv_up]
        gvs[:, s0:s0 + spc] = o["gvo"].reshape(L, spc, 8, 768)
        if r == cpb - 1:
            wkp[:, b] = o["kpo"].transpose(0, 2, 1).reshape(L, 128, 4, 64)
            wvp[:, b] = o["vpo"].reshape(L, 128, 4, 64)
            zp = o["zpo"].transpose(0, 3, 2, 1).reshape(L, 2, 5632)
            cvp[:, b] = zp[:, :, inv_up]
    return (y_prompt, y_sample, wkp, wvp, cvp, wks, wvs, cvs_, gvs)
```
